# Optimizing a Trainium2 kernel written in Bass

```python
import jax, jax.numpy as jnp
from jax import lax
import numpy as np

D_MODEL = 4096
BATCH = 2
SEQ = 4096
DEPTH = 4

GRID_W = 64
N_HEADS = 16
HEAD_DIM = 128
ATTN_W = N_HEADS * HEAD_DIM
WIN_R = 8
WIN_C = 16
POOL_WINDOWS = (2, 4, 8, 16)
POOL_GROUP = 256
POOL_W = POOL_GROUP * len(POOL_WINDOWS)
GMLP_CHUNK = 128
GMLP_GROUPS = 4
GMLP_GROUP_W = 256
GMLP_W = GMLP_GROUPS * GMLP_GROUP_W
MIX_W = ATTN_W + POOL_W + GMLP_W
IN_W = 3 * ATTN_W + POOL_W + 2 * GMLP_W
N_BRANCH = 3
GATE_RANK = 512
D_FF = (8 * D_MODEL + 3 * 256 - 1) // (3 * 256) * 256
EPS = 1e-6

kernel_name = "hybrid_natten_pool_gmlp_encoder"


def rms_norm(x, g):
    xf = x.astype(jnp.float32)
    y = xf * lax.rsqrt(jnp.mean(xf * xf, axis=-1, keepdims=True) + EPS)
    return (y * g.astype(jnp.float32)).astype(x.dtype)


def layer_norm(x, g, b):
    xf = x.astype(jnp.float32)
    mu = jnp.mean(xf, axis=-1, keepdims=True)
    xc = xf - mu
    y = xc * lax.rsqrt(jnp.mean(xc * xc, axis=-1, keepdims=True) + EPS)
    return (y * g.astype(jnp.float32) + b.astype(jnp.float32)).astype(x.dtype)


def neighbourhood_attention(q, k, v, rpb):
    B, T, H, hd = q.shape
    rows = T // GRID_W
    wr = min(WIN_R, rows)
    r = jnp.arange(rows)
    row_start = jnp.clip(r - wr // 2, 0, rows - wr)
    row_idx = row_start[:, None] + jnp.arange(wr)[None, :]
    c = jnp.arange(GRID_W)
    col_start = jnp.clip(c - WIN_C // 2, 0, GRID_W - WIN_C)
    col_valid = (c[None, :] >= col_start[:, None]) & (c[None, :] < col_start[:, None] + WIN_C)
    ro = row_idx - r[:, None] + (WIN_R - 1)
    co = jnp.clip(c[None, :] - c[:, None], -(WIN_C - 1), WIN_C - 1) + (WIN_C - 1)
    bias = rpb[:, ro[:, None, :, None], co[None, :, None, :]]
    qg = q.reshape(B, rows, GRID_W, H, hd)
    kg = k.reshape(B, rows, GRID_W, H, hd)[:, row_idx]
    vg = v.reshape(B, rows, GRID_W, H, hd)[:, row_idx]
    s = jnp.einsum('brqhd,brwkhd->bhrqwk', qg, kg).astype(jnp.float32) * (HEAD_DIM ** -0.5)
    s = s + bias.astype(jnp.float32)[None]
    s = jnp.where(col_valid[:, None, :], s, -jnp.inf)
    p = jax.nn.softmax(s.reshape(B, H, rows, GRID_W, wr * GRID_W), axis=-1)
    p = p.reshape(s.shape).astype(v.dtype)
    o = jnp.einsum('bhrqwk,brwkhd->brqhd', p, vg)
    return o.reshape(B, T, H * hd)


def multiscale_pool(p, w_grp, scale):
    T = p.shape[1]
    pf = p.astype(jnp.float32)
    cs = jnp.concatenate([jnp.zeros_like(pf[:, :1]), jnp.cumsum(pf, axis=1)], axis=1)
    t = jnp.arange(T)
    outs = []
    for g, w in enumerate(POOL_WINDOWS):
        sl = slice(g * POOL_GROUP, (g + 1) * POOL_GROUP)
        lo = jnp.clip(t - w // 2, 0, T - 1)
        hi = jnp.clip(t - w // 2 + w - 1, 0, T - 1)
        csg = cs[..., sl]
        cnt = (hi - lo + 1).astype(jnp.float32)[None, :, None]
        mean = (jnp.take(csg, hi + 1, axis=1) - jnp.take(csg, lo, axis=1)) / cnt
        d = (mean - pf[..., sl]).astype(p.dtype)
        outs.append(d @ w_grp[g])
    return jnp.concatenate(outs, axis=-1) * scale


def spatial_gating(uv, ln_g, ln_b, w_s, b_s):
    uv = jax.nn.gelu(uv)
    u, v = uv[..., :GMLP_W], uv[..., GMLP_W:]
    v = layer_norm(v, ln_g, ln_b)
    B, T, _ = v.shape
    nc = T // GMLP_CHUNK
    vc = v.reshape(B, nc, GMLP_CHUNK, GMLP_GROUPS, GMLP_GROUP_W)
    mixed = jnp.einsum('gij,bcjgd->bcigd', w_s, vc) + b_s.T[None, None, :, :, None]
    return u * mixed.reshape(B, T, GMLP_W)


def setup_inputs(seed: int = 0) -> dict:
    key = jax.random.key(seed)
    ks = jax.random.split(key, 20)
    L, D = DEPTH, D_MODEL
    nrm = jax.random.normal
    return {
        "x": nrm(ks[0], (BATCH, SEQ, D), jnp.float32),
        "attn_norm_g": 1.0 + 0.02 * nrm(ks[1], (L, D), jnp.float32),
        "w_in": nrm(ks[2], (L, D, IN_W), jnp.float32) * D ** -0.5,
        "rpb": 0.1 * nrm(ks[3], (L, N_HEADS, 2 * WIN_R - 1, 2 * WIN_C - 1), jnp.float32),
        "pool_w": nrm(ks[4], (L, len(POOL_WINDOWS), POOL_GROUP, POOL_GROUP), jnp.float32) * POOL_GROUP ** -0.5,
        "pool_scale": 1.0 + 0.02 * nrm(ks[5], (L, POOL_W), jnp.float32),
        "gmlp_ln_g": 1.0 + 0.02 * nrm(ks[6], (L, GMLP_W), jnp.float32),
        "gmlp_ln_b": 0.02 * nrm(ks[7], (L, GMLP_W), jnp.float32),
        "gmlp_w_s": nrm(ks[8], (L, GMLP_GROUPS, GMLP_CHUNK, GMLP_CHUNK), jnp.float32) * GMLP_CHUNK ** -0.5,
        "gmlp_b_s": 0.02 * nrm(ks[9], (L, GMLP_GROUPS, GMLP_CHUNK), jnp.float32),
        "w_branch": nrm(ks[10], (L, MIX_W, D), jnp.float32) * (MIX_W // 4) ** -0.5,
        "gate_down": nrm(ks[11], (L, D, GATE_RANK), jnp.float32) * D ** -0.5,
        "gate_up": nrm(ks[12], (L, GATE_RANK, N_BRANCH * D), jnp.float32) * GATE_RANK ** -0.5,
        "gate_b": 0.02 * nrm(ks[13], (L, N_BRANCH * D), jnp.float32),
        "w_out": nrm(ks[14], (L, D, D), jnp.float32) * D ** -0.5,
        "ffn_norm_g": 1.0 + 0.02 * nrm(ks[15], (L, D), jnp.float32),
        "w_ffn_gate": nrm(ks[16], (L, D, D_FF), jnp.float32) * D ** -0.5,
        "w_ffn_up": nrm(ks[17], (L, D, D_FF), jnp.float32) * D ** -0.5,
        "w_ffn_down": nrm(ks[18], (L, D_FF, D), jnp.float32) * D_FF ** -0.5,
        "final_norm_g": 1.0 + 0.02 * nrm(ks[19], (D,), jnp.float32),
    }


def reference(x, attn_norm_g, w_in, rpb, pool_w, pool_scale, gmlp_ln_g, gmlp_ln_b, gmlp_w_s,
              gmlp_b_s, w_branch, gate_down, gate_up, gate_b, w_out, ffn_norm_g, w_ffn_gate,
              w_ffn_up, w_ffn_down, final_norm_g):
    B, T, D = x.shape
    o_k = ATTN_W
    o_v = 2 * ATTN_W
    o_p = 3 * ATTN_W
    o_g = 3 * ATTN_W + POOL_W
    for l in range(DEPTH):
        h = rms_norm(x, attn_norm_g[l])
        z = h @ w_in[l]
        q = z[..., :o_k].reshape(B, T, N_HEADS, HEAD_DIM)
        k = z[..., o_k:o_v].reshape(B, T, N_HEADS, HEAD_DIM)
        v = z[..., o_v:o_p].reshape(B, T, N_HEADS, HEAD_DIM)
        y_attn = neighbourhood_attention(q, k, v, rpb[l])
        y_pool = multiscale_pool(z[..., o_p:o_g], pool_w[l], pool_scale[l])
        y_sg = spatial_gating(z[..., o_g:], gmlp_ln_g[l], gmlp_ln_b[l], gmlp_w_s[l], gmlp_b_s[l])
        wb = w_branch[l]
        b_attn = y_attn @ wb[:ATTN_W]
        b_pool = y_pool @ wb[ATTN_W:ATTN_W + POOL_W]
        b_sg = y_sg @ wb[ATTN_W + POOL_W:]
        gates = jax.nn.sigmoid((h @ gate_down[l]) @ gate_up[l] + gate_b[l]).reshape(B, T, N_BRANCH, D)
        merged = gates[:, :, 0] * b_attn + gates[:, :, 1] * b_pool + gates[:, :, 2] * b_sg
        x = x + merged @ w_out[l]
        h = rms_norm(x, ffn_norm_g[l])
        x = x + (jax.nn.silu(h @ w_ffn_gate[l]) * (h @ w_ffn_up[l])) @ w_ffn_down[l]
    return rms_norm(x, final_norm_g)
```

```python
import numpy as np
from contextlib import ExitStack
import concourse.bass as bass
import concourse.mybir as mybir
from concourse.bass_utils import run_bass_kernel_spmd

F32 = mybir.dt.float32
BF16 = mybir.dt.bfloat16
AF = mybir.ActivationFunctionType
ALU = mybir.AluOpType
AX = mybir.AxisListType

D = 4096
T = 4096
NG = 4
GT = 1024
TT = 512
KC = 32
NJ = 86
DOWN_SPLIT = (32, 32, 22)
NTILES = 26 + 12 + 22 + 16 + 86 + 48
POOL_WINDOWS = (2, 4, 8, 16)
EPS = 1e-6

AG, FG, GB, PS = 0, 32, 64, 160
LNG, LNB, BSB = 168, 1192, 2216
PW, WST, BIAS = 4264, 6312, 6824
NS = BIAS + 16 * 960
CID, CINV, CFNG, CMASK = 0, 128, 128 + 4 * 4096, 128 + 4 * 4096 + 32
NCST = CMASK + 960


class Ev:
    __slots__ = ("sem", "sid", "val")

    def __init__(self, sem, sid, val):
        self.sem, self.sid, self.val = sem, sid, val


class Buf:
    __slots__ = ("w", "r")

    def __init__(self):
        self.w = None
        self.r = {}


class KB:
    def __init__(self, nc):
        self.nc = nc
        self.eng = {"pe": nc.tensor, "act": nc.scalar, "dve": nc.vector, "sp": nc.sync, "pool": nc.gpsimd}
        self.csem = {}
        self.waited = {}
        self.nsem = 0
        self.new_epoch()
        self.dsem = {}
        for q, n in (("sp", 24), ("pool", 6)):
            self.dsem[q] = [[self._sem() for _ in range(n)], [0] * n, 0]

    def _sem(self):
        self.nsem += 1
        return (self.nc.alloc_semaphore("s%d" % self.nsem), self.nsem)

    def new_epoch(self):
        for e in ("pe", "act", "dve"):
            self.csem[e] = [self._sem(), 0]

    def _wait(self, eng, ev):
        if ev is None:
            return
        if eng == "pe" and ev.sid == self.csem["pe"][0][1]:
            return
        key = (eng, ev.sid)
        if self.waited.get(key, 0) >= ev.val:
            return
        self.eng[eng].wait_ge(ev.sem, ev.val)
        self.waited[key] = ev.val

    def _deps(self, eng, reads, writes):
        for b in reads:
            self._wait(eng, b.w)
        for b in writes:
            self._wait(eng, b.w)
            for ev in b.r.values():
                self._wait(eng, ev)

    def _post(self, ev, reads, writes):
        for b in reads:
            old = b.r.get(ev.sid)
            if old is None or old.val < ev.val:
                b.r[ev.sid] = ev
        for b in writes:
            b.w = ev
            b.r = {}

    def op(self, eng, fn, reads=(), writes=()):
        self._deps(eng, reads, writes)
        ins = fn()
        s = self.csem[eng]
        s[1] += 1
        ins.then_inc(s[0][0], 1)
        ev = Ev(s[0][0], s[0][1], s[1])
        self._post(ev, reads, writes)
        return ev

    def dma(self, q, out, in_, reads=(), writes=()):
        pool = self.dsem[q]
        i = pool[2]
        pool[2] = (i + 1) % len(pool[0])
        sem, sid = pool[0][i]
        if pool[1][i] > 0:
            self._wait(q, Ev(sem, sid, pool[1][i]))
        self._deps(q, reads, writes)
        ins = self.eng[q].dma_start(out=out, in_=in_)
        pool[1][i] += 16
        ins.then_inc(sem, 16)
        ev = Ev(sem, sid, pool[1][i])
        self._post(ev, reads, writes)
        return ev

    def drain(self, eng, bufs):
        for b in bufs:
            self._wait(eng, b.w)


class Ring:
    NS = 4

    def __init__(self, kb, tiles):
        self.kb = kb
        self.tiles = tiles
        nc = kb.nc
        self.slots = [nc.alloc_sbuf_tensor("ring%d" % i, [128, 8192], BF16) for i in range(self.NS)]
        self.bufs = [Buf() for _ in range(self.NS)]
        self.next_dma = 0
        self.next_acq = 0
        self.released = [False] * len(tiles)

    def _pump(self):
        n = self.NS
        while (self.next_dma < len(self.tiles) and self.next_dma < self.next_acq + n - 1 + 1
               and (self.next_dma < n or self.released[self.next_dma - n])):
            i = self.next_dma
            ap, width, tag = self.tiles[i]
            self.kb.dma("pool", self.slots[i % n][:, 0:width], ap, writes=[self.bufs[i % n]])
            self.next_dma += 1

    def acquire(self, tag):
        i = self.next_acq
        assert self.tiles[i][2] == tag, (self.tiles[i][2], tag)
        self.next_acq += 1
        self._pump()
        assert self.next_dma > i, "ring deadlock at tile %d" % i
        return i, self.slots[i % self.NS], self.bufs[i % self.NS]

    def release(self, i):
        self.released[i] = True
        self._pump()


def layer_tile_list(wl, l):
    a, rest = [], []
    row = 0

    def t(width, tag):
        nonlocal row
        ap = wl[row * 128:(row + 1) * 128, 0:width]
        row += 1
        return (ap, width, tag)

    for i in range(26):
        a.append(t(8192, ("fm", l, i)))
    for i in range(12):
        a.append(t(8192, ("tm", l, i)))
    for m in range(32):
        if m % 2 == 0:
            rest.append(t(8192, ("br", l, m // 2)))
        for br in range(3):
            i = m * 3 + br
            if i % 16 == 0:
                rest.append(t(8192, ("gu", l, i // 16)))
    for i in range(16):
        rest.append(t(8192, ("wo", l, i)))
    for j in range(NJ):
        rest.append(t(8192, ("ff", l, j)))
    for s, n in enumerate(DOWN_SPLIT):
        for mp in range(16):
            rest.append(t(2 * n * 128, ("dn", l, s, mp)))
    assert row == NTILES
    return a, rest


def build_program(n_layers=4, debug_out=None):
    nc = bass.Bass("TRN2", target_bir_lowering=False)
    kb = KB(nc)
    dram = nc.dram_tensor
    x_in = dram("xT", [D, T], F32, kind="ExternalInput")
    wls = [dram("w%d" % l, [NTILES * 128, 8192], F32, kind="ExternalInput") for l in range(n_layers)]
    sps = [dram("sp%d" % l, [128, NS], F32, kind="ExternalInput") for l in range(n_layers)]
    cst = dram("cst", [128, NCST], F32, kind="ExternalInput")
    outT = dram("outT", [D, T], F32, kind="ExternalOutput")
    xs = dram("xs", [D, T], F32)
    qT = dram("qT", [2048, T], BF16)
    kT = dram("kT", [2048, T], BF16)
    Vd = dram("Vd", [T, 2048], BF16)
    pT = dram("pT", [1024, T + 16], F32)
    uT = dram("uT", [1024, T], BF16)
    vn = dram("vn", [T, 1024], BF16)
    hgT = dram("hgT", [512, T], BF16)
    yT = dram("yT", [D, T], BF16)
    mT = dram("mT", [D, T], BF16)
    actT = dram("actT", [NJ * 128, GT], BF16)
    dbg = {}
    if debug_out:
        for name in debug_out:
            src = {"yT": yT, "mT": mT, "xs": xs, "qT": qT, "kT": kT, "Vd": Vd, "uT": uT, "vn": vn, "hgT": hgT, "pT": pT}[name]
            dbg[name] = (dram("dbg_" + name, list(src.shape), src.dtype, kind="ExternalOutput"), src)

    def gb(n=NG):
        return [Buf() for _ in range(n)]
    B_x = [[[Buf() for _ in range(2)] for _ in range(NG)] for _ in range(KC)]
    B_q, B_k, B_V, B_p, B_u, B_vn, B_hg = gb(), gb(), gb(), gb(), gb(), gb(), gb()
    B_y = [[Buf() for _ in range(NG)] for _ in range(KC)]
    B_m = [[Buf() for _ in range(NG)] for _ in range(KC)]
    B_act = [Buf() for _ in range(NJ)]
    B_ppad = Buf()

    tiles = []
    for l in range(n_layers):
        a, rest = layer_tile_list(wls[l], l)
        for g in range(NG):
            tiles += a
        for g in range(NG):
            tiles += rest
    ring = Ring(kb, tiles)

    sb = nc.alloc_sbuf_tensor
    _uq = [0]

    def uq(name):
        _uq[0] += 1
        return "%s_%d" % (name, _uq[0])
    big = sb("big", [128, KC, GT], BF16)
    big_b = [[Buf() for _ in range(2)] for _ in range(KC)]
    psum = [nc.alloc_psum_tensor("ps%d" % i, [128, 512], F32) for i in range(7)]
    psum_b = [Buf() for _ in range(7)]
    psT = nc.alloc_psum_tensor("psT", [128, 1024], BF16)
    psT_b = [Buf() for _ in range(4)]
    psO_b = [Buf(), Buf()]
    pctr = [0]

    def bank():
        i = pctr[0] % 6
        pctr[0] += 1
        return psum[i], psum_b[i]

    def rot(name, n, shape, dt):
        ts = [sb("%s%d" % (name, i), shape, dt) for i in range(n)]
        bs = [Buf() for _ in range(n)]
        ctr = [0]

        def nxt():
            i = ctr[0] % n
            ctr[0] += 1
            return ts[i], bs[i]
        return nxt

    xst = rot("xst", 3, [128, TT], F32)
    sqt = rot("sq", 2, [128, TT], BF16)
    rstdt = rot("rstd", 2, [128, TT], F32)
    ost = rot("ost", 4, [128, TT], BF16)
    ostf = rot("ostf", 3, [128, TT], F32)
    accs = [sb("acc%d" % i, [128, TT], F32) for i in range(2)]
    acc_b = [Buf(), Buf()]

    ident = sb("ident", [128, 128], BF16)
    ones = sb("ones", [128, 128], BF16)
    fng = sb("fng", [128, 32], F32)
    maskt = sb("maskt", [64, 960], F32)
    spc = sb("spc", [128, 168], F32)
    pwb = sb("pwb", [128, 2048], BF16)
    wstb = sb("wstb", [128, 512], BF16)
    B_const, B_spc = Buf(), Buf()
    kb.dma("pool", ident[:, :], cst[:, CID:CID + 128], writes=[B_const])
    kb.dma("sp", fng[:, :], cst[:, CFNG:CFNG + 32], writes=[B_const])
    kb.dma("sp", maskt[:, :], cst[0:64, CMASK:CMASK + 960], writes=[B_const])
    kb.op("dve", lambda: nc.vector.memset(ones[:, :], 1.0), writes=[B_const])
    zt, zb = ostf()
    kb.op("dve", lambda: nc.vector.memset(zt[:, :], 0.0), writes=[zb])
    for cc in range(8):
        kb.dma("sp", pT[cc * 128:(cc + 1) * 128, 0:8], zt[:, 0:8], reads=[zb], writes=[B_ppad])
        kb.dma("sp", pT[cc * 128:(cc + 1) * 128, T + 8:T + 16], zt[:, 0:8], reads=[zb], writes=[B_ppad])

    mm = nc.tensor.matmul

    def barrier():
        evs = []
        for e in ("pe", "act", "dve"):
            s = kb.csem[e]
            if s[1] > 0:
                evs.append(Ev(s[0][0], s[0][1], s[1]))
        pool = kb.dsem["sp"]
        for (sem, sid), c in zip(pool[0], pool[1]):
            if c > 0:
                evs.append(Ev(sem, sid, c))
        for e in ("pe", "act", "dve", "sp"):
            for ev in evs:
                if e == "pe" and ev.sid == kb.csem["pe"][0][1]:
                    continue
                kb._wait(e, ev)

    def rms_to(x_src, g_ap, g, dst_fn):
        T0 = g * GT
        for t in range(2):
            pb, pbb = bank()
            c0 = T0 + t * TT
            for kc in range(KC):
                xt, xb = xst()
                kb.dma("sp", xt[:, :], x_src[kc * 128:(kc + 1) * 128, c0:c0 + TT], reads=[B_x[kc][g][t]], writes=[xb])
                sq, sqb = sqt()
                kb.op("act", lambda: nc.scalar.activation(out=sq[:, :], in_=xt[:, :], func=AF.Square), reads=[xb], writes=[sqb])
                mm_group(pb[:, :], pbb, [(ones[:, :], sq[:, :])], [sqb, B_const], first=(kc == 0), last=(kc == KC - 1))
            rs, rsb = rstdt()
            kb.op("act", lambda: nc.scalar.activation(out=rs[:, :], in_=pb[:, :], func=AF.Sqrt, bias=EPS, scale=1.0 / D),
                  reads=[pbb], writes=[rsb])
            kb.op("dve", lambda: nc.vector.reciprocal(out=rs[:, :], in_=rs[:, :]), reads=[rsb], writes=[rsb])
            for kc in range(KC):
                xt, xb = xst()
                kb.dma("sp", xt[:, :], x_src[kc * 128:(kc + 1) * 128, c0:c0 + TT], reads=[B_x[kc][g][t]], writes=[xb])
                dst_fn(kc, t, xt, xb, g_ap[:, kc:kc + 1], rs, rsb)

    def norm_to_big(x_src, g_ap, g):
        def f(kc, t, xt, xb, gcol, rs, rsb):
            kb.op("dve", lambda: nc.vector.scalar_tensor_tensor(out=big[:, kc, t * TT:(t + 1) * TT], in0=xt[:, :], scalar=gcol,
                                                                in1=rs[:, :], op0=ALU.mult, op1=ALU.mult),
                  reads=[xb, rsb, B_spc], writes=[big_b[kc][t]])
        rms_to(x_src, g_ap, g, f)

    def mm_group(pb, pbb, pairs, reads, first=True, last=True):
        kb._deps("pe", reads, [pbb])
        n = len(pairs)
        ins = None
        for i, (l_ap, r_ap) in enumerate(pairs):
            ins = mm(pb, lhsT=l_ap, rhs=r_ap, start=(first and i == 0), stop=(last and i == n - 1))
        s = kb.csem["pe"]
        s[1] += 1
        ins.then_inc(s[0][0], 1)
        ev = Ev(s[0][0], s[0][1], s[1])
        kb._post(ev, reads, [pbb])
        return ev

    def fm_chunk(slot, sbuf_, off, nk, rhs_of, rhs_bufs, evac):
        for t in range(2):
            pb, pbb = bank()
            pairs = [(slot[:, (off + kc) * 128:(off + kc + 1) * 128], rhs_of(kc, t)) for kc in range(nk)]
            mm_group(pb[:, :], pbb, pairs, [sbuf_] + [rhs_bufs(kc, t) for kc in range(nk)])
            evac(t, pb, pbb)

    big_rhs = lambda kc, t: big[:, kc, t * TT:(t + 1) * TT]
    big_rb = lambda kc, t: big_b[kc][t]

    def phase_A(l, g):
        T0 = g * GT
        x_src = x_in if l == 0 else xs
        sp_ = sps[l]
        with ExitStack() as es:
            vst = es.enter_context(nc.sbuf_tensor(uq("vst"), [128, 8, 1024], F32))
            lng = es.enter_context(nc.sbuf_tensor(uq("lng"), [128, 1024], F32))
            lnb = es.enter_context(nc.sbuf_tensor(uq("lnb"), [128, 1024], F32))
            stats = es.enter_context(nc.sbuf_tensor(uq("stats"), [128, 8, 12], F32))
            mv = es.enter_context(nc.sbuf_tensor(uq("mv"), [128, 8, 2], F32))
            B_ln = Buf()
            vst_b = [[Buf() for _ in range(4)] for _ in range(8)]
            kb.dma("sp", lng[:, :], sp_[:, LNG:LNG + 1024], writes=[B_ln])
            kb.dma("sp", lnb[:, :], sp_[:, LNB:LNB + 1024], writes=[B_ln])
            norm_to_big(x_src, spc[:, AG:AG + 32], g)
            for ti in range(26):
                ri, slot, sbuf_ = ring.acquire(("fm", l, ti))
                for o in range(2):
                    j = 2 * ti + o

                    def evac(t, pb, pbb, j=j):
                        c0 = T0 + t * TT
                        if 32 <= j < 40:
                            st, stb = ostf()
                            kb.op("act", lambda: nc.scalar.copy(out=st[:, :], in_=pb[:, :]), reads=[pbb], writes=[stb])
                            kb.dma("sp", pT[(j - 32) * 128:(j - 31) * 128, 8 + c0:8 + c0 + TT], st[:, :], reads=[stb], writes=[B_p[g]])
                            return
                        st, stb = ost()
                        if 40 <= j < 48:
                            kb.op("act", lambda: nc.scalar.activation(out=st[:, :], in_=pb[:, :], func=AF.Gelu_apprx_tanh),
                                  reads=[pbb], writes=[stb])
                            dst, db = uT[(j - 40) * 128:(j - 39) * 128, c0:c0 + TT], B_u[g]
                        else:
                            kb.op("dve", lambda: nc.vector.tensor_copy(out=st[:, :], in_=pb[:, :]), reads=[pbb], writes=[stb])
                            if j < 16:
                                dst, db = qT[j * 128:(j + 1) * 128, c0:c0 + TT], B_q[g]
                            elif j < 32:
                                dst, db = kT[(j - 16) * 128:(j - 15) * 128, c0:c0 + TT], B_k[g]
                            else:
                                dst, db = hgT[(j - 48) * 128:(j - 47) * 128, c0:c0 + TT], B_hg[g]
                        kb.dma("sp", dst, st[:, :], reads=[stb], writes=[db])
                    fm_chunk(slot, sbuf_, o * KC, KC, big_rhs, big_rb, evac)
                ring.release(ri)
            for cg in range(12):
                ri, slot, sbuf_ = ring.acquire(("tm", l, cg))
                for c in range(8):
                    pb, pbb = bank()
                    t, cq = c // 4, c % 4
                    pairs = [(big[:, kc, c * 128:(c + 1) * 128], slot[:, kc * 256:(kc + 1) * 256]) for kc in range(KC)]
                    mm_group(pb[:, 0:256], pbb, pairs, [sbuf_] + [big_b[kc][t] for kc in range(KC)])
                    if cg < 8:
                        st, stb = ost()
                        kb.op("dve", lambda: nc.vector.tensor_copy(out=st[:, 0:256], in_=pb[:, 0:256]), reads=[pbb], writes=[stb])
                        kb.dma("sp", Vd[T0 + c * 128:T0 + (c + 1) * 128, cg * 256:(cg + 1) * 256], st[:, 0:256], reads=[stb], writes=[B_V[g]])
                    else:
                        q4 = cg - 8
                        kb.op("act", lambda: nc.scalar.activation(out=vst[:, c, q4 * 256:(q4 + 1) * 256], in_=pb[:, 0:256],
                                                                  func=AF.Gelu_apprx_tanh), reads=[pbb], writes=[vst_b[c][q4]])
                ring.release(ri)
            for c in range(8):
                vb = vst_b[c]
                sb_ = Buf()
                for hlf in range(2):
                    kb.op("dve", lambda: nc.vector.bn_stats(out=stats[:, c, hlf * 6:(hlf + 1) * 6], in_=vst[:, c, hlf * 512:(hlf + 1) * 512]),
                          reads=vb, writes=[sb_])
                kb.op("dve", lambda: nc.vector.bn_aggr(out=mv[:, c, :], in_=stats[:, c, :]), reads=[sb_], writes=[sb_])
                kb.op("act", lambda: nc.scalar.activation(out=mv[:, c, 1:2], in_=mv[:, c, 1:2], func=AF.Sqrt, bias=EPS, scale=1.0),
                      reads=[sb_], writes=[sb_])
                kb.op("dve", lambda: nc.vector.reciprocal(out=mv[:, c, 1:2], in_=mv[:, c, 1:2]), reads=[sb_], writes=[sb_])
                kb.op("dve", lambda: nc.vector.tensor_scalar(out=vst[:, c, :], in0=vst[:, c, :], scalar1=mv[:, c, 0:1],
                                                             scalar2=mv[:, c, 1:2], op0=ALU.subtract, op1=ALU.mult),
                      reads=[sb_] + vb, writes=vb)
                kb.op("dve", lambda: nc.vector.tensor_tensor(out=vst[:, c, :], in0=vst[:, c, :], in1=lng[:, :], op=ALU.mult),
                      reads=vb + [B_ln], writes=vb)
                for hlf in range(2):
                    st, stb = ost()
                    kb.op("dve", lambda: nc.vector.tensor_tensor(out=st[:, :], in0=vst[:, c, hlf * 512:(hlf + 1) * 512],
                                                                 in1=lnb[:, hlf * 512:(hlf + 1) * 512], op=ALU.add),
                          reads=vb + [B_ln], writes=[stb])
                    kb.dma("sp", vn[T0 + c * 128:T0 + (c + 1) * 128, hlf * 512:(hlf + 1) * 512], st[:, :], reads=[stb], writes=[B_vn[g]])
            barrier()

    def phase_attn(l, g):
        T0 = g * GT
        lo = max(16 * g - 4, 0)
        hi = min(16 * g + 19, 63)
        nrows = hi - lo + 1
        ntok = nrows * 64
        ne, no = nrows // 2, (nrows - 1) // 2
        scale = 128.0 ** -0.5
        kdeps = [B_k[gg] for gg in range(NG) if not (16 * gg + 15 < lo or 16 * gg > hi)]
        vdeps = [B_V[gg] for gg in range(NG) if not (16 * gg + 15 < lo or 16 * gg > hi)]
        with ExitStack() as es:
            def two(name, shape, dt):
                return [es.enter_context(nc.sbuf_tensor(uq("%s%d" % (name, i)), shape, dt)) for i in range(2)]
            Kh = two("Kh", [128, 1536], BF16)
            qh = two("qh", [128, GT], BF16)
            Ve = two("Ve", [128, 12, 128], BF16)
            Vo = two("Vo", [128, 12, 128], BF16)
            Bm1 = es.enter_context(nc.sbuf_tensor(uq("Bm"), [64, 960], F32))
            Bm = [Bm1, Bm1]
            ssb = two("ssb", [64, 512], F32)
            psb = two("psb", [64, 512], F32)
            pnb = two("pnb", [64, 512], BF16)
            pTs = two("pTs", [128, 256], BF16)
            yh = two("yh", [128, GT], BF16)
            st4 = two("st4", [64, 4], F32)
            hb = [[Buf() for _ in range(6)] for _ in range(2)]
            hb[1][4] = hb[0][4]
            wb = [[Buf() for _ in range(5)] for _ in range(2)]
            it = 0
            for h in range(16):
                hp = h % 2
                bK, bq, bVe, bVo, bBm, byh = hb[hp]
                kb.dma("sp", Kh[hp][:, 0:ntok], kT[h * 128:(h + 1) * 128, lo * 64:lo * 64 + ntok], reads=kdeps, writes=[bK])
                kb.dma("sp", qh[hp][:, :], qT[h * 128:(h + 1) * 128, T0:T0 + GT], reads=[B_q[g]], writes=[bq])
                kb.dma("sp", Ve[hp][:, 0:ne, :],
                       Vd[lo * 64:lo * 64 + ne * 128, h * 128:(h + 1) * 128].rearrange("(j p) d -> p j d", p=128),
                       reads=vdeps, writes=[bVe])
                kb.dma("sp", Vo[hp][:, 0:no, :],
                       Vd[lo * 64 + 64:lo * 64 + 64 + no * 128, h * 128:(h + 1) * 128].rearrange("(j p) d -> p j d", p=128),
                       reads=vdeps, writes=[bVo])
                kb.dma("sp", Bm[hp][:, :], sps[l][0:64, BIAS + h * 960:BIAS + (h + 1) * 960], writes=[bBm])
                kb.op("dve", lambda: nc.vector.tensor_tensor(out=Bm[hp][:, :], in0=Bm[hp][:, :], in1=maskt[:, :], op=ALU.add),
                      reads=[bBm, B_const], writes=[bBm])
                for i in range(16):
                    r = 16 * g + i
                    rs_ = min(max(r - 4, 0), 56)
                    e0 = rs_ - lo
                    ro0 = rs_ - r + 7
                    ip = it % 2
                    it += 1
                    bs, bp, bpn, bpT, bst = wb[ip]
                    pb, pbb = bank()
                    mm_group(pb[0:64, :], pbb, [(qh[hp][:, i * 64:(i + 1) * 64], Kh[hp][:, e0 * 64:e0 * 64 + 512])], [bq, bK])
                    s_, p_, pn_, pT_, st_ = ssb[ip], psb[ip], pnb[ip], pTs[ip], st4[ip]
                    kb.op("dve", lambda: nc.vector.scalar_tensor_tensor(out=s_[:, :], in0=pb[0:64, :], scalar=scale,
                                                                        in1=Bm[hp][:, ro0 * 64:ro0 * 64 + 512],
                                                                        op0=ALU.mult, op1=ALU.add),
                          reads=[pbb, bBm], writes=[bs])
                    kb.op("dve", lambda: nc.vector.reduce_max(out=st_[:, 0:1], in_=s_[:, :], axis=AX.X), reads=[bs], writes=[bst])
                    kb.op("dve", lambda: nc.vector.tensor_scalar(out=st_[:, 1:2], in0=st_[:, 0:1], scalar1=-1.0, scalar2=None,
                                                                 op0=ALU.mult), reads=[bst], writes=[bst])
                    kb.op("act", lambda: nc.scalar.activation(out=p_[:, :], in_=s_[:, :], func=AF.Exp, bias=st_[:, 1:2], scale=1.0),
                          reads=[bs, bst], writes=[bp])
                    kb.op("dve", lambda: nc.vector.reduce_sum(out=st_[:, 2:3], in_=p_[:, :], axis=AX.X), reads=[bp], writes=[bst])
                    kb.op("dve", lambda: nc.vector.reciprocal(out=st_[:, 3:4], in_=st_[:, 2:3]), reads=[bst], writes=[bst])
                    kb.op("dve", lambda: nc.vector.tensor_scalar(out=pn_[:, :], in0=p_[:, :], scalar1=st_[:, 3:4], scalar2=None,
                                                                 op0=ALU.mult), reads=[bp, bst], writes=[bpn])
                    tb = psT_b[ip]
                    kb._deps("pe", [bpn, B_const], [tb])
                    ins = None
                    for j in range(4):
                        ins = nc.tensor.transpose(out=psT[:, ip * 256 + j * 64:ip * 256 + (j + 1) * 64],
                                                  in_=pn_[:, j * 128:(j + 1) * 128], identity=ident[0:64, 0:64])
                    s = kb.csem["pe"]
                    s[1] += 1
                    ins.then_inc(s[0][0], 1)
                    ev = Ev(s[0][0], s[0][1], s[1])
                    kb._post(ev, [bpn, B_const], [tb])
                    kb.op("act", lambda: nc.scalar.copy(out=pT_[:, :], in_=psT[:, ip * 256:(ip + 1) * 256]), reads=[tb], writes=[bpT])
                    if e0 % 2 == 0:
                        Vx, bVx, j0 = Ve[hp], bVe, e0 // 2
                    else:
                        Vx, bVx, j0 = Vo[hp], bVo, (e0 - 1) // 2
                    ob, obb = psum[6], psO_b[ip]
                    mm_group(ob[:, ip * 64:(ip + 1) * 64], obb,
                             [(Vx[:, j0 + j, :], pT_[:, j * 64:(j + 1) * 64]) for j in range(4)], [bVx, bpT])
                    kb.op("act", lambda: nc.scalar.copy(out=yh[hp][:, i * 64:(i + 1) * 64], in_=ob[:, ip * 64:(ip + 1) * 64]),
                          reads=[obb], writes=[byh])
                kb.dma("sp", yT[h * 128:(h + 1) * 128, T0:T0 + GT], yh[hp][:, :], reads=[byh], writes=[B_y[h][g]])
            barrier()

    def phase_pool(l, g):
        T0 = g * GT
        pdeps = [B_p[gg] for gg in (g - 1, g, g + 1) if 0 <= gg < NG] + [B_ppad]
        with ExitStack() as es:
            invc = es.enter_context(nc.sbuf_tensor(uq("invc"), [128, GT], F32))
            pe_ = [es.enter_context(nc.sbuf_tensor(uq("pe%d" % i), [128, 1040], F32)) for i in range(2)]
            sA = es.enter_context(nc.sbuf_tensor(uq("sA"), [128, 1040], F32))
            sB = es.enter_context(nc.sbuf_tensor(uq("sB"), [128, 1040], F32))
            dch = es.enter_context(nc.sbuf_tensor(uq("dch"), [128, 8, GT], BF16))
            binv, bsA, bsB = Buf(), Buf(), Buf()
            bpe = [Buf(), Buf()]
            bd = [Buf() for _ in range(8)]
            for cc in range(8):
                wi = cc // 2
                w = POOL_WINDOWS[wi]
                pt, pb_ = pe_[cc % 2], bpe[cc % 2]
                if cc % 2 == 0:
                    kb.dma("sp", invc[:, :], cst[:, CINV + wi * 4096 + T0:CINV + wi * 4096 + T0 + GT], writes=[binv])
                kb.dma("sp", pt[:, :], pT[cc * 128:(cc + 1) * 128, T0:T0 + 1040], reads=pdeps, writes=[pb_])
                src, sbf = pt, pb_
                L = 1040
                step = 1
                flip = 0
                while step < w:
                    dst, dbf = (sA, bsA) if flip == 0 else (sB, bsB)
                    flip ^= 1
                    L2 = L - step
                    kb.op("dve", lambda: nc.vector.tensor_tensor(out=dst[:, 0:L2], in0=src[:, 0:L2], in1=src[:, step:step + L2], op=ALU.add),
                          reads=[sbf], writes=[dbf])
                    src, sbf, L = dst, dbf, L2
                    step *= 2
                o0 = 8 - w // 2
                dst, dbf = (sA, bsA) if flip == 0 else (sB, bsB)
                kb.op("dve", lambda: nc.vector.tensor_tensor(out=dst[:, 0:GT], in0=src[:, o0:o0 + GT], in1=invc[:, :], op=ALU.mult),
                      reads=[sbf, binv], writes=[dbf])
                kb.op("dve", lambda: nc.vector.tensor_tensor(out=dch[:, cc, :], in0=dst[:, 0:GT], in1=pt[:, 8:8 + GT], op=ALU.subtract),
                      reads=[dbf, pb_], writes=[bd[cc]])
            for pg in range(4):
                for m in range(2):
                    oc = pg * 2 + m
                    for t in range(2):
                        pb, pbb = bank()
                        pairs = [(pwb[:, ((pg * 2 + kc) * 2 + m) * 128:((pg * 2 + kc) * 2 + m + 1) * 128],
                                  dch[:, pg * 2 + kc, t * TT:(t + 1) * TT]) for kc in range(2)]
                        mm_group(pb[:, :], pbb, pairs, [B_spc, bd[pg * 2], bd[pg * 2 + 1]])
                        st, stb = ost()
                        kb.op("dve", lambda: nc.vector.tensor_scalar(out=st[:, :], in0=pb[:, :], scalar1=spc[:, PS + oc:PS + oc + 1], scalar2=None,
                                                                     op0=ALU.mult), reads=[pbb, B_spc], writes=[stb])
                        kb.dma("sp", yT[2048 + oc * 128:2048 + (oc + 1) * 128, T0 + t * TT:T0 + (t + 1) * TT], st[:, :],
                               reads=[stb], writes=[B_y[16 + oc][g]])
            barrier()

    def phase_sg(l, g):
        T0 = g * GT
        with ExitStack() as es:
            vnb = es.enter_context(nc.sbuf_tensor(uq("vnb"), [128, 8, 1024], BF16))
            bsb = es.enter_context(nc.sbuf_tensor(uq("bsb"), [128, 2048], F32))
            ut = [es.enter_context(nc.sbuf_tensor(uq("ut%d" % i), [128, TT], BF16)) for i in range(2)]
            bvn = [Buf() for _ in range(8)]
            bbs = Buf()
            but = [Buf(), Buf()]
            kb.dma("sp", bsb[:, :], sps[l][:, BSB:BSB + 2048], writes=[bbs])
            for c in range(8):
                kb.dma("sp", vnb[:, c, :], vn[T0 + c * 128:T0 + (c + 1) * 128, :], reads=[B_vn[g]], writes=[bvn[c]])
            n = 0
            for ch in range(8):
                sgg = ch // 2
                for t in range(2):
                    u_, ub = ut[n % 2], but[n % 2]
                    n += 1
                    kb.dma("sp", u_[:, :], uT[ch * 128:(ch + 1) * 128, T0 + t * TT:T0 + (t + 1) * TT], reads=[B_u[g]], writes=[ub])
                    pb, pbb = bank()
                    kb._deps("pe", [bvn[t * 4 + cq] for cq in range(4)] + [B_spc], [pbb])
                    ins = None
                    for cq in range(4):
                        c = t * 4 + cq
                        ins = mm(pb[:, cq * 128:(cq + 1) * 128], lhsT=vnb[:, c, ch * 128:(ch + 1) * 128],
                                 rhs=wstb[:, sgg * 128:(sgg + 1) * 128], start=True, stop=True)
                    s = kb.csem["pe"]
                    s[1] += 1
                    ins.then_inc(s[0][0], 1)
                    ev = Ev(s[0][0], s[0][1], s[1])
                    kb._post(ev, [bvn[t * 4 + cq] for cq in range(4)] + [B_spc], [pbb])
                    tf, tfb = ostf()
                    kb.op("dve", lambda: nc.vector.tensor_tensor(out=tf[:, :], in0=pb[:, :], in1=bsb[:, sgg * 512:(sgg + 1) * 512], op=ALU.add),
                          reads=[pbb, bbs], writes=[tfb])
                    st, stb = ost()
                    kb.op("dve", lambda: nc.vector.tensor_tensor(out=st[:, :], in0=tf[:, :], in1=u_[:, :], op=ALU.mult),
                          reads=[tfb, ub], writes=[stb])
                    kb.dma("sp", yT[3072 + ch * 128:3072 + (ch + 1) * 128, T0 + t * TT:T0 + (t + 1) * TT], st[:, :],
                           reads=[stb], writes=[B_y[24 + ch][g]])
            barrier()

    def load_big(src, bufs, g, nk=KC, row0=0):
        T0 = g * GT
        for kc in range(nk):
            for t in range(2):
                kb.dma("sp", big[:, kc, t * TT:(t + 1) * TT], src[(row0 + kc) * 128:(row0 + kc + 1) * 128, T0 + t * TT:T0 + (t + 1) * TT],
                       reads=[bufs(kc)], writes=[big_b[kc][t]])

    def phase_C(l, g):
        T0 = g * GT
        with ExitStack() as es:
            hgb = es.enter_context(nc.sbuf_tensor(uq("hgb"), [128, 4, GT], BF16))
            bhg = Buf()
            for kc in range(4):
                kb.dma("sp", hgb[:, kc, :], hgT[kc * 128:(kc + 1) * 128, T0:T0 + GT], reads=[B_hg[g]], writes=[bhg])
            load_big(yT, lambda kc: B_y[kc][g], g)
            bri = gui = None
            for m in range(32):
                if m % 2 == 0:
                    if bri is not None:
                        ring.release(bri[0])
                    bri = ring.acquire(("br", l, m // 2))
                o = m % 2
                for br in range(3):
                    i = m * 3 + br
                    if i % 16 == 0:
                        if gui is not None:
                            ring.release(gui[0])
                        gui = ring.acquire(("gu", l, i // 16))
                    goff = (i % 16) * 4
                    ks = (range(0, 16), range(16, 24), range(24, 32))[br]
                    for t in range(2):
                        gp, gpb = bank()
                        mm_group(gp[:, :], gpb, [(gui[1][:, (goff + kc) * 128:(goff + kc + 1) * 128], hgb[:, kc, t * TT:(t + 1) * TT])
                                                 for kc in range(4)], [gui[2], bhg])
                        bp_, bpb = bank()
                        mm_group(bp_[:, :], bpb, [(bri[1][:, (o * KC + kc) * 128:(o * KC + kc + 1) * 128], big[:, kc, t * TT:(t + 1) * TT])
                                                  for kc in ks], [bri[2]] + [big_b[kc][t] for kc in ks])
                        sg_, sgb = ostf()
                        kb.op("act", lambda: nc.scalar.activation(out=sg_[:, :], in_=gp[:, :], func=AF.Sigmoid,
                                                                  bias=spc[:, GB + br * 32 + m:GB + br * 32 + m + 1], scale=1.0),
                              reads=[gpb, B_spc], writes=[sgb])
                        if br == 0:
                            kb.op("dve", lambda: nc.vector.tensor_tensor(out=accs[t][:, :], in0=bp_[:, :], in1=sg_[:, :], op=ALU.mult),
                                  reads=[bpb, sgb], writes=[acc_b[t]])
                        else:
                            kb.op("dve", lambda: nc.vector.tensor_tensor(out=sg_[:, :], in0=bp_[:, :], in1=sg_[:, :], op=ALU.mult),
                                  reads=[bpb, sgb], writes=[sgb])
                            if br == 1:
                                kb.op("dve", lambda: nc.vector.tensor_tensor(out=accs[t][:, :], in0=accs[t][:, :], in1=sg_[:, :], op=ALU.add),
                                      reads=[acc_b[t], sgb], writes=[acc_b[t]])
                            else:
                                st, stb = ost()
                                kb.op("dve", lambda: nc.vector.tensor_tensor(out=st[:, :], in0=accs[t][:, :], in1=sg_[:, :], op=ALU.add),
                                      reads=[acc_b[t], sgb], writes=[stb])
                                kb.dma("sp", mT[m * 128:(m + 1) * 128, T0 + t * TT:T0 + (t + 1) * TT], st[:, :], reads=[stb], writes=[B_m[m][g]])
            ring.release(bri[0])
            ring.release(gui[0])
            barrier()

    def resid_evac(l, g, m, x_src):
        T0 = g * GT

        def evac(t, pb, pbb):
            c0 = T0 + t * TT
            xt, xb = xst()
            kb.dma("sp", xt[:, :], x_src[m * 128:(m + 1) * 128, c0:c0 + TT], reads=[B_x[m][g][t]], writes=[xb])
            kb.op("dve", lambda: nc.vector.tensor_tensor(out=xt[:, :], in0=pb[:, :], in1=xt[:, :], op=ALU.add), reads=[pbb, xb], writes=[xb])
            kb.dma("sp", xs[m * 128:(m + 1) * 128, c0:c0 + TT], xt[:, :], reads=[xb], writes=[B_x[m][g][t]])
        return evac

    def phase_D(l, g):
        load_big(mT, lambda kc: B_m[kc][g], g)
        x_src = x_in if l == 0 else xs
        for ti in range(16):
            ri, slot, sbuf_ = ring.acquire(("wo", l, ti))
            for o in range(2):
                fm_chunk(slot, sbuf_, o * KC, KC, big_rhs, big_rb, resid_evac(l, g, 2 * ti + o, x_src))
            ring.release(ri)

    def phase_E(l, g):
        norm_to_big(xs, spc[:, FG:FG + 32], g)
        for j in range(NJ):
            ri, slot, sbuf_ = ring.acquire(("ff", l, j))
            for t in range(2):
                gp, gpb = bank()
                mm_group(gp[:, :], gpb, [(slot[:, kc * 128:(kc + 1) * 128], big[:, kc, t * TT:(t + 1) * TT]) for kc in range(KC)],
                         [sbuf_] + [big_b[kc][t] for kc in range(KC)])
                up, upb = bank()
                mm_group(up[:, :], upb, [(slot[:, (KC + kc) * 128:(KC + kc + 1) * 128], big[:, kc, t * TT:(t + 1) * TT]) for kc in range(KC)],
                         [sbuf_] + [big_b[kc][t] for kc in range(KC)])
                sg_, sgb = ostf()
                kb.op("act", lambda: nc.scalar.activation(out=sg_[:, :], in_=gp[:, :], func=AF.Silu), reads=[gpb], writes=[sgb])
                st, stb = ost()
                kb.op("dve", lambda: nc.vector.tensor_tensor(out=st[:, :], in0=up[:, :], in1=sg_[:, :], op=ALU.mult),
                      reads=[upb, sgb], writes=[stb])
                kb.dma("sp", actT[j * 128:(j + 1) * 128, t * TT:(t + 1) * TT], st[:, :], reads=[stb], writes=[B_act[j]])
            ring.release(ri)

    def phase_F(l, g):
        k0 = 0
        for s, n in enumerate(DOWN_SPLIT):
            for kk in range(n):
                for t in range(2):
                    kb.dma("sp", big[:, kk, t * TT:(t + 1) * TT], actT[(k0 + kk) * 128:(k0 + kk + 1) * 128, t * TT:(t + 1) * TT],
                           reads=[B_act[k0 + kk]], writes=[big_b[kk][t]])
            for mp in range(16):
                ri, slot, sbuf_ = ring.acquire(("dn", l, s, mp))
                for o in range(2):
                    fm_chunk(slot, sbuf_, o * n, n, big_rhs, big_rb, resid_evac(l, g, 2 * mp + o, xs))
                ring.release(ri)
            k0 += n

    def phase_final(g):
        def f(kc, t, xt, xb, gcol, rs, rsb):
            kb.op("dve", lambda: nc.vector.scalar_tensor_tensor(out=xt[:, :], in0=xt[:, :], scalar=gcol, in1=rs[:, :],
                                                                op0=ALU.mult, op1=ALU.mult), reads=[xb, rsb, B_const], writes=[xb])
            kb.dma("sp", outT[kc * 128:(kc + 1) * 128, g * GT + t * TT:g * GT + (t + 1) * TT], xt[:, :], reads=[xb], writes=[B_out])
        rms_to(xs, fng, g, f)

    B_out = Buf()
    for l in range(n_layers):
        if l > 0:
            kb.new_epoch()
        sp_ = sps[l]
        kb.dma("sp", spc[:, :], sp_[:, 0:168], writes=[B_spc])
        kb.dma("pool", pwb[:, :], sp_[:, PW:PW + 2048], writes=[B_spc])
        kb.dma("pool", wstb[:, :], sp_[:, WST:WST + 512], writes=[B_spc])
        for g in range(NG):
            phase_A(l, g)
        for g in range(NG):
            phase_attn(l, g)
            phase_pool(l, g)
            phase_sg(l, g)
            phase_C(l, g)
            phase_D(l, g)
            phase_E(l, g)
            phase_F(l, g)
    for g in range(NG):
        phase_final(g)
    for name, (dst, src) in dbg.items():
        barrier()
        nr = src.shape[0]
        for r0 in range(0, nr, 1024):
            r1 = min(nr, r0 + 1024)
            kb.dma("sp", dst[r0:r1, :], src[r0:r1, :], writes=[B_out])
    pool = kb.dsem["sp"]
    for (sem, sid), c in zip(pool[0], pool[1]):
        if c > 0:
            kb._wait("sp", Ev(sem, sid, c))
    for e in ("pe", "act", "dve"):
        s = kb.csem[e]
        if s[1] > 0:
            kb._wait("sp", Ev(s[0][0], s[0][1], s[1]))
    return nc


def _tiles_fm(M, cols, KCn=KC):
    K = M.shape[0]
    assert K == KCn * 128
    idx = np.concatenate([np.arange(c, c + 128) for c in cols])
    sub = M[:, idx]
    nch = len(cols)
    a = sub.reshape(KCn, 128, nch // 2, 2, 128).transpose(2, 1, 3, 0, 4)
    return np.ascontiguousarray(a).reshape(nch // 2 * 128, 2 * KCn * 128)


def _layer_weights(inp, l):
    out = np.zeros((NTILES * 128, 8192), np.float32)
    r = 0

    def put(a):
        nonlocal r
        out[r:r + a.shape[0], :a.shape[1]] = a
        r += a.shape[0]
    w_in = inp["w_in"][l]
    gd = inp["gate_down"][l]
    cols = [j * 128 for j in range(32)] + [6144 + j * 128 for j in range(8)] + [7168 + j * 128 for j in range(8)]
    put(_tiles_fm(w_in, cols))
    put(_tiles_fm(gd, [j * 128 for j in range(4)]))
    for cg in range(12):
        c0 = 4096 + cg * 256 if cg < 8 else 8192 + (cg - 8) * 256
        a = w_in[:, c0:c0 + 256].reshape(KC, 128, 256).transpose(1, 0, 2)
        put(np.ascontiguousarray(a).reshape(128, KC * 256))
    wb = _tiles_fm(inp["w_branch"][l], [m * 128 for m in range(32)])
    gu = inp["gate_up"][l]
    gidx = np.concatenate([np.arange(br * 4096 + m * 128, br * 4096 + m * 128 + 128) for m in range(32) for br in range(3)])
    gsub = gu[:, gidx].reshape(4, 128, 6, 16, 128).transpose(2, 1, 3, 0, 4)
    gsub = np.ascontiguousarray(gsub).reshape(6 * 128, 8192)
    for m in range(32):
        if m % 2 == 0:
            put(wb[(m // 2) * 128:(m // 2 + 1) * 128])
        for br in range(3):
            i = m * 3 + br
            if i % 16 == 0:
                put(gsub[(i // 16) * 128:(i // 16 + 1) * 128])
    put(_tiles_fm(inp["w_out"][l], [m * 128 for m in range(32)]))
    wg, wu = inp["w_ffn_gate"][l], inp["w_ffn_up"][l]
    a = np.stack([wg.reshape(KC, 128, NJ, 128), wu.reshape(KC, 128, NJ, 128)], 0).transpose(3, 2, 0, 1, 4)
    put(np.ascontiguousarray(a).reshape(NJ * 128, 8192))
    wd = inp["w_ffn_down"][l]
    k0 = 0
    for n in DOWN_SPLIT:
        put(_tiles_fm(wd[k0 * 128:(k0 + n) * 128], [m * 128 for m in range(32)], n))
        k0 += n
    assert r == NTILES * 128
    return out


def _layer_small(inp, l):
    sp = np.zeros((128, NS), np.float32)
    fm = lambda v: v.reshape(-1, 128).T
    sp[:, AG:AG + 32] = fm(inp["attn_norm_g"][l])
    sp[:, FG:FG + 32] = fm(inp["ffn_norm_g"][l])
    sp[:, GB:GB + 96] = fm(inp["gate_b"][l])
    sp[:, PS:PS + 8] = fm(inp["pool_scale"][l])
    sp[:, LNG:LNG + 1024] = inp["gmlp_ln_g"][l][None, :]
    sp[:, LNB:LNB + 1024] = inp["gmlp_ln_b"][l][None, :]
    bs = inp["gmlp_b_s"][l]
    sp[:, BSB:BSB + 2048] = np.tile(bs[:, None, :], (1, 4, 1)).reshape(1, 2048)
    pw = inp["pool_w"][l]
    sp[:, PW:PW + 2048] = pw.reshape(4, 2, 128, 2, 128).transpose(2, 0, 1, 3, 4).reshape(128, 2048)
    ws = inp["gmlp_w_s"][l]
    sp[:, WST:WST + 512] = ws.transpose(2, 0, 1).reshape(128, 512)
    rpb = inp["rpb"][l]
    cq = np.arange(64)[:, None]
    ck = np.arange(64)[None, :]
    co = np.clip(ck - cq, -15, 15) + 15
    bias = rpb[:, :, co]
    sp[0:64, BIAS:BIAS + 16 * 960] = bias.transpose(2, 0, 1, 3).reshape(64, 16 * 960)
    return sp


def _consts(inp):
    c = np.zeros((128, NCST), np.float32)
    c[:, CID:CID + 128] = np.eye(128, dtype=np.float32)
    t = np.arange(T)
    for wi, w in enumerate(POOL_WINDOWS):
        lo = np.clip(t - w // 2, 0, T - 1)
        hi = np.clip(t - w // 2 + w - 1, 0, T - 1)
        c[:, CINV + wi * T:CINV + (wi + 1) * T] = (1.0 / (hi - lo + 1).astype(np.float32))[None, :]
    c[:, CFNG:CFNG + 32] = inp["final_norm_g"].reshape(-1, 128).T
    cq = np.arange(64)[:, None]
    ck = np.arange(64)[None, :]
    cs = np.clip(cq - 8, 0, 48)
    valid = (ck >= cs) & (ck < cs + 16)
    m = np.where(valid, 0.0, -30000.0).astype(np.float32)
    c[0:64, CMASK:CMASK + 960] = np.tile(m[:, None, :], (1, 15, 1)).reshape(64, 960)
    return c


_CACHE = {}


def kernel(**inputs):
    inp = {k: np.asarray(v) for k, v in inputs.items()}
    n_layers = inp["w_in"].shape[0]
    ncores = inp["x"].shape[0]
    if "nc" not in _CACHE:
        _CACHE["nc"] = build_program(n_layers)
    nc = _CACHE["nc"]
    shared = {"cst": _consts(inp)}
    for l in range(n_layers):
        shared["w%d" % l] = _layer_weights(inp, l)
        shared["sp%d" % l] = _layer_small(inp, l)
    in_maps = []
    for c in range(ncores):
        m = dict(shared)
        m["xT"] = np.ascontiguousarray(inp["x"][c].T)
        in_maps.append(m)
    res = run_bass_kernel_spmd(nc, in_maps, core_ids=list(range(ncores)))
    out = np.stack([np.ascontiguousarray(np.asarray(res.results[c]["outT"]).T) for c in range(ncores)], 0)
    return out.astype(np.float32)
```

```python
import numpy as np
from contextlib import ExitStack
import concourse.bass as bass
import concourse.mybir as mybir
from concourse.bass_utils import run_bass_kernel_spmd

F32 = mybir.dt.float32
BF16 = mybir.dt.bfloat16
AF = mybir.ActivationFunctionType
ALU = mybir.AluOpType
AX = mybir.AxisListType

D = 4096
T = 4096
NG = 4
GT = 1024
TT = 512
KC = 32
NJ = 86
DOWN_SPLIT = (32, 32, 22)
NTILES = 26 + 12 + 22 + 16 + 86 + 48
POOL_WINDOWS = (2, 4, 8, 16)
EPS = 1e-6

AG, FG, GB, PS = 0, 32, 64, 160
LNG, LNB, BSB = 168, 1192, 2216
PW, WST, BIAS = 4264, 6312, 6824
NS = BIAS + 16 * 960
CID, CINV, CFNG, CMASK = 0, 128, 128 + 4 * 4096, 128 + 4 * 4096 + 32
NCST = CMASK + 960


class Ev:
    __slots__ = ("sem", "sid", "val")

    def __init__(self, sem, sid, val):
        self.sem, self.sid, self.val = sem, sid, val


class Buf:
    __slots__ = ("w", "r")

    def __init__(self):
        self.w = None
        self.r = {}


class KB:
    def __init__(self, nc):
        self.nc = nc
        self.eng = {"pe": nc.tensor, "act": nc.scalar, "dve": nc.vector, "sp": nc.sync, "pool": nc.gpsimd}
        self.csem = {}
        self.waited = {}
        self.nsem = 0
        self.new_epoch()
        self.dsem = {}
        for q, n in (("sp", 24), ("pool", 6)):
            self.dsem[q] = [[self._sem() for _ in range(n)], [0] * n, 0]

    def _sem(self):
        self.nsem += 1
        return (self.nc.alloc_semaphore("s%d" % self.nsem), self.nsem)

    def new_epoch(self):
        for e in ("pe", "act", "dve"):
            self.csem[e] = [self._sem(), 0]

    def _wait(self, eng, ev):
        if ev is None:
            return
        if eng == "pe" and ev.sid == self.csem["pe"][0][1]:
            return
        key = (eng, ev.sid)
        if self.waited.get(key, 0) >= ev.val:
            return
        self.eng[eng].wait_ge(ev.sem, ev.val)
        self.waited[key] = ev.val

    def _deps(self, eng, reads, writes):
        for b in reads:
            self._wait(eng, b.w)
        for b in writes:
            self._wait(eng, b.w)
            for ev in b.r.values():
                self._wait(eng, ev)

    def _post(self, ev, reads, writes):
        for b in reads:
            old = b.r.get(ev.sid)
            if old is None or old.val < ev.val:
                b.r[ev.sid] = ev
        for b in writes:
            b.w = ev
            b.r = {}

    def op(self, eng, fn, reads=(), writes=()):
        self._deps(eng, reads, writes)
        ins = fn()
        s = self.csem[eng]
        s[1] += 1
        ins.then_inc(s[0][0], 1)
        ev = Ev(s[0][0], s[0][1], s[1])
        self._post(ev, reads, writes)
        return ev

    def dma(self, q, out, in_, reads=(), writes=()):
        pool = self.dsem[q]
        i = pool[2]
        pool[2] = (i + 1) % len(pool[0])
        sem, sid = pool[0][i]
        if pool[1][i] > 0:
            self._wait(q, Ev(sem, sid, pool[1][i]))
        self._deps(q, reads, writes)
        ins = self.eng[q].dma_start(out=out, in_=in_)
        pool[1][i] += 16
        ins.then_inc(sem, 16)
        ev = Ev(sem, sid, pool[1][i])
        self._post(ev, reads, writes)
        return ev

    def drain(self, eng, bufs):
        for b in bufs:
            self._wait(eng, b.w)


class Ring:
    NS = 4

    def __init__(self, kb, tiles):
        self.kb = kb
        self.tiles = tiles
        nc = kb.nc
        self.slots = [nc.alloc_sbuf_tensor("ring%d" % i, [128, 8192], BF16) for i in range(self.NS)]
        self.bufs = [Buf() for _ in range(self.NS)]
        self.next_dma = 0
        self.next_acq = 0
        self.released = [False] * len(tiles)

    def _pump(self):
        n = self.NS
        while (self.next_dma < len(self.tiles) and self.next_dma < self.next_acq + n - 1 + 1
               and (self.next_dma < n or self.released[self.next_dma - n])):
            i = self.next_dma
            ap, width, tag = self.tiles[i]
            self.kb.dma("pool", self.slots[i % n][:, 0:width], ap, writes=[self.bufs[i % n]])
            self.next_dma += 1

    def acquire(self, tag):
        i = self.next_acq
        assert self.tiles[i][2] == tag, (self.tiles[i][2], tag)
        self.next_acq += 1
        self._pump()
        assert self.next_dma > i, "ring deadlock at tile %d" % i
        return i, self.slots[i % self.NS], self.bufs[i % self.NS]

    def release(self, i):
        self.released[i] = True
        self._pump()


def layer_tile_list(wl, l):
    a, rest = [], []
    row = 0

    def t(width, tag):
        nonlocal row
        ap = wl[row * 128:(row + 1) * 128, 0:width]
        row += 1
        return (ap, width, tag)

    for i in range(26):
        a.append(t(8192, ("fm", l, i)))
    for i in range(12):
        a.append(t(8192, ("tm", l, i)))
    for m in range(32):
        if m % 2 == 0:
            rest.append(t(8192, ("br", l, m // 2)))
        for br in range(3):
            i = m * 3 + br
            if i % 16 == 0:
                rest.append(t(8192, ("gu", l, i // 16)))
    for i in range(16):
        rest.append(t(8192, ("wo", l, i)))
    for j in range(NJ):
        rest.append(t(8192, ("ff", l, j)))
    for s, n in enumerate(DOWN_SPLIT):
        for mp in range(16):
            rest.append(t(2 * n * 128, ("dn", l, s, mp)))
    assert row == NTILES
    return a, rest


def build_program(n_layers=4, debug_out=None, mode=None, flags=()):
    nc = bass.Bass("TRN2", target_bir_lowering=False)
    kb = KB(nc)
    dram = nc.dram_tensor
    x_in = dram("xT", [D, T], F32, kind="ExternalInput")
    wls = [dram("w%d" % l, [NTILES * 128, 8192] if mode is None else [NTILES * 128, 64], F32, kind=("ExternalInput" if mode is None else "Internal")) for l in range(n_layers)]
    sps = [dram("sp%d" % l, [128, NS], F32, kind="ExternalInput") for l in range(n_layers)]
    cst = dram("cst", [128, NCST], F32, kind="ExternalInput")
    outT = dram("outT", [D, T], F32, kind="ExternalOutput")
    xs = dram("xs", [D, T], F32)
    qT = dram("qT", [2048, T], BF16)
    kT = dram("kT", [2048, T], BF16)
    Vd = dram("Vd", [T, 2048], BF16)
    pT = dram("pT", [1024, T + 16], F32)
    uT = dram("uT", [1024, T], BF16)
    vn = dram("vn", [T, 1024], BF16)
    hgT = dram("hgT", [512, T], BF16)
    yT = dram("yT", [D, T], BF16)
    mT = dram("mT", [D, T], BF16)
    actT = dram("actT", [NJ * 128, GT], BF16)
    dbg = {}
    if debug_out:
        for name in debug_out:
            src = {"yT": yT, "mT": mT, "xs": xs, "qT": qT, "kT": kT, "Vd": Vd, "uT": uT, "vn": vn, "hgT": hgT, "pT": pT}[name]
            dbg[name] = (dram("dbg_" + name, list(src.shape), src.dtype, kind="ExternalOutput"), src)

    def gb(n=NG):
        return [Buf() for _ in range(n)]
    B_x = [[[Buf() for _ in range(2)] for _ in range(NG)] for _ in range(KC)]
    B_q, B_k, B_V, B_p, B_u, B_vn, B_hg = gb(), gb(), gb(), gb(), gb(), gb(), gb()
    B_y = [[Buf() for _ in range(NG)] for _ in range(KC)]
    B_m = [[Buf() for _ in range(NG)] for _ in range(KC)]
    B_act = [Buf() for _ in range(NJ)]
    B_ppad = Buf()

    tiles = []
    for l in range(n_layers if mode is None else 0):
        a, rest = layer_tile_list(wls[l], l)
        for g in range(NG):
            tiles += a
        for g in range(NG):
            tiles += rest
    ring = Ring(kb, tiles)

    sb = nc.alloc_sbuf_tensor
    _uq = [0]

    def uq(name):
        _uq[0] += 1
        return "%s_%d" % (name, _uq[0])
    big = sb("big", [128, KC, GT], BF16)
    big_b = [[Buf() for _ in range(2)] for _ in range(KC)]
    psum = [nc.alloc_psum_tensor("ps%d" % i, [128, 512], F32) for i in range(6)]
    psum_b = [Buf() for _ in range(6)]
    psTs = [nc.alloc_psum_tensor("psT%d" % i, [128, 1024], BF16) for i in range(2)]
    psT_b = [Buf() for _ in range(2)]
    pctr = [0]

    def bank():
        i = pctr[0] % 6
        pctr[0] += 1
        return psum[i], psum_b[i]

    def rot(name, n, shape, dt):
        ts = [sb("%s%d" % (name, i), shape, dt) for i in range(n)]
        bs = [Buf() for _ in range(n)]
        ctr = [0]

        def nxt():
            i = ctr[0] % n
            ctr[0] += 1
            return ts[i], bs[i]
        return nxt

    xst = rot("xst", 3, [128, TT], F32)
    sqt = rot("sq", 2, [128, TT], BF16)
    rstdt = rot("rstd", 2, [128, TT], F32)
    ost = rot("ost", 3, [128, TT], BF16)
    ostf = rot("ostf", 3, [128, TT], F32)
    accs = [sb("acc%d" % i, [128, TT], F32) for i in range(2)]
    acc_b = [Buf(), Buf()]

    ident = sb("ident", [128, 128], BF16)
    ones = sb("ones", [128, 128], BF16)
    fng = sb("fng", [128, 32], F32)
    maskt = sb("maskt", [64, 960], F32)
    spc = sb("spc", [128, 168], F32)
    pwb = sb("pwb", [128, 2048], BF16)
    wstb = sb("wstb", [128, 512], BF16)
    B_const, B_spc = Buf(), Buf()
    kb.dma("pool", ident[:, :], cst[:, CID:CID + 128], writes=[B_const])
    kb.dma("sp", fng[:, :], cst[:, CFNG:CFNG + 32], writes=[B_const])
    kb.dma("sp", maskt[:, :], cst[0:64, CMASK:CMASK + 960], writes=[B_const])
    kb.op("dve", lambda: nc.vector.memset(ones[:, :], 1.0), writes=[B_const])
    zt, zb = ostf()
    kb.op("dve", lambda: nc.vector.memset(zt[:, :], 0.0), writes=[zb])
    for cc in range(8):
        kb.dma("sp", pT[cc * 128:(cc + 1) * 128, 0:8], zt[:, 0:8], reads=[zb], writes=[B_ppad])
        kb.dma("sp", pT[cc * 128:(cc + 1) * 128, T + 8:T + 16], zt[:, 0:8], reads=[zb], writes=[B_ppad])

    mm = nc.tensor.matmul

    def barrier():
        evs = []
        for e in ("pe", "act", "dve"):
            s = kb.csem[e]
            if s[1] > 0:
                evs.append(Ev(s[0][0], s[0][1], s[1]))
        pool = kb.dsem["sp"]
        for (sem, sid), c in zip(pool[0], pool[1]):
            if c > 0:
                evs.append(Ev(sem, sid, c))
        for e in ("pe", "act", "dve", "sp"):
            for ev in evs:
                if e == "pe" and ev.sid == kb.csem["pe"][0][1]:
                    continue
                kb._wait(e, ev)

    def rms_to(x_src, g_ap, g, dst_fn):
        T0 = g * GT
        for t in range(2):
            pb, pbb = bank()
            c0 = T0 + t * TT
            for kc in range(KC):
                xt, xb = xst()
                kb.dma("sp", xt[:, :], x_src[kc * 128:(kc + 1) * 128, c0:c0 + TT], reads=[B_x[kc][g][t]], writes=[xb])
                sq, sqb = sqt()
                kb.op("act", lambda: nc.scalar.activation(out=sq[:, :], in_=xt[:, :], func=AF.Square), reads=[xb], writes=[sqb])
                mm_group(pb[:, :], pbb, [(ones[:, :], sq[:, :])], [sqb, B_const], first=(kc == 0), last=(kc == KC - 1))
            rs, rsb = rstdt()
            kb.op("act", lambda: nc.scalar.activation(out=rs[:, :], in_=pb[:, :], func=AF.Sqrt, bias=EPS, scale=1.0 / D),
                  reads=[pbb], writes=[rsb])
            kb.op("dve", lambda: nc.vector.reciprocal(out=rs[:, :], in_=rs[:, :]), reads=[rsb], writes=[rsb])
            for kc in range(KC):
                xt, xb = xst()
                kb.dma("sp", xt[:, :], x_src[kc * 128:(kc + 1) * 128, c0:c0 + TT], reads=[B_x[kc][g][t]], writes=[xb])
                dst_fn(kc, t, xt, xb, g_ap[:, kc:kc + 1], rs, rsb)

    def norm_to_big(x_src, g_ap, g):
        def f(kc, t, xt, xb, gcol, rs, rsb):
            kb.op("dve", lambda: nc.vector.scalar_tensor_tensor(out=big[:, kc, t * TT:(t + 1) * TT], in0=xt[:, :], scalar=gcol,
                                                                in1=rs[:, :], op0=ALU.mult, op1=ALU.mult),
                  reads=[xb, rsb, B_spc], writes=[big_b[kc][t]])
        rms_to(x_src, g_ap, g, f)

    def mm_group(pb, pbb, pairs, reads, first=True, last=True):
        kb._deps("pe", reads, [pbb])
        n = len(pairs)
        ins = None
        for i, (l_ap, r_ap) in enumerate(pairs):
            ins = mm(pb, lhsT=l_ap, rhs=r_ap, start=(first and i == 0), stop=(last and i == n - 1))
        s = kb.csem["pe"]
        s[1] += 1
        ins.then_inc(s[0][0], 1)
        ev = Ev(s[0][0], s[0][1], s[1])
        kb._post(ev, reads, [pbb])
        return ev

    def fm_chunk(slot, sbuf_, off, nk, rhs_of, rhs_bufs, evac):
        for t in range(2):
            pb, pbb = bank()
            pairs = [(slot[:, (off + kc) * 128:(off + kc + 1) * 128], rhs_of(kc, t)) for kc in range(nk)]
            mm_group(pb[:, :], pbb, pairs, [sbuf_] + [rhs_bufs(kc, t) for kc in range(nk)])
            evac(t, pb, pbb)

    big_rhs = lambda kc, t: big[:, kc, t * TT:(t + 1) * TT]
    big_rb = lambda kc, t: big_b[kc][t]

    def phase_A(l, g):
        T0 = g * GT
        x_src = x_in if l == 0 else xs
        sp_ = sps[l]
        with ExitStack() as es:
            vst = es.enter_context(nc.sbuf_tensor(uq("vst"), [128, 8, 1024], F32))
            lng = es.enter_context(nc.sbuf_tensor(uq("lng"), [128, 1024], F32))
            lnb = es.enter_context(nc.sbuf_tensor(uq("lnb"), [128, 1024], F32))
            stats = es.enter_context(nc.sbuf_tensor(uq("stats"), [128, 8, 12], F32))
            mv = es.enter_context(nc.sbuf_tensor(uq("mv"), [128, 8, 2], F32))
            B_ln = Buf()
            vst_b = [[Buf() for _ in range(4)] for _ in range(8)]
            kb.dma("sp", lng[:, :], sp_[:, LNG:LNG + 1024], writes=[B_ln])
            kb.dma("sp", lnb[:, :], sp_[:, LNB:LNB + 1024], writes=[B_ln])
            norm_to_big(x_src, spc[:, AG:AG + 32], g)
            for ti in range(26):
                ri, slot, sbuf_ = ring.acquire(("fm", l, ti))
                for o in range(2):
                    j = 2 * ti + o

                    def evac(t, pb, pbb, j=j):
                        c0 = T0 + t * TT
                        if 32 <= j < 40:
                            st, stb = ostf()
                            kb.op("act", lambda: nc.scalar.copy(out=st[:, :], in_=pb[:, :]), reads=[pbb], writes=[stb])
                            kb.dma("sp", pT[(j - 32) * 128:(j - 31) * 128, 8 + c0:8 + c0 + TT], st[:, :], reads=[stb], writes=[B_p[g]])
                            return
                        st, stb = ost()
                        if 40 <= j < 48:
                            kb.op("act", lambda: nc.scalar.activation(out=st[:, :], in_=pb[:, :], func=AF.Gelu_apprx_tanh),
                                  reads=[pbb], writes=[stb])
                            dst, db = uT[(j - 40) * 128:(j - 39) * 128, c0:c0 + TT], B_u[g]
                        else:
                            kb.op("dve", lambda: nc.vector.tensor_copy(out=st[:, :], in_=pb[:, :]), reads=[pbb], writes=[stb])
                            if j < 16:
                                dst, db = qT[j * 128:(j + 1) * 128, c0:c0 + TT], B_q[g]
                            elif j < 32:
                                dst, db = kT[(j - 16) * 128:(j - 15) * 128, c0:c0 + TT], B_k[g]
                            else:
                                dst, db = hgT[(j - 48) * 128:(j - 47) * 128, c0:c0 + TT], B_hg[g]
                        kb.dma("sp", dst, st[:, :], reads=[stb], writes=[db])
                    fm_chunk(slot, sbuf_, o * KC, KC, big_rhs, big_rb, evac)
                ring.release(ri)
            for cg in range(12):
                ri, slot, sbuf_ = ring.acquire(("tm", l, cg))
                for c in range(8):
                    pb, pbb = bank()
                    t, cq = c // 4, c % 4
                    pairs = [(big[:, kc, c * 128:(c + 1) * 128], slot[:, kc * 256:(kc + 1) * 256]) for kc in range(KC)]
                    mm_group(pb[:, 0:256], pbb, pairs, [sbuf_] + [big_b[kc][t] for kc in range(KC)])
                    if cg < 8:
                        st, stb = ost()
                        kb.op("dve", lambda: nc.vector.tensor_copy(out=st[:, 0:256], in_=pb[:, 0:256]), reads=[pbb], writes=[stb])
                        kb.dma("sp", Vd[T0 + c * 128:T0 + (c + 1) * 128, cg * 256:(cg + 1) * 256], st[:, 0:256], reads=[stb], writes=[B_V[g]])
                    else:
                        q4 = cg - 8
                        kb.op("act", lambda: nc.scalar.activation(out=vst[:, c, q4 * 256:(q4 + 1) * 256], in_=pb[:, 0:256],
                                                                  func=AF.Gelu_apprx_tanh), reads=[pbb], writes=[vst_b[c][q4]])
                ring.release(ri)
            for c in range(8):
                vb = vst_b[c]
                sb_ = Buf()
                for hlf in range(2):
                    kb.op("dve", lambda: nc.vector.bn_stats(out=stats[:, c, hlf * 6:(hlf + 1) * 6], in_=vst[:, c, hlf * 512:(hlf + 1) * 512]),
                          reads=vb, writes=[sb_])
                kb.op("dve", lambda: nc.vector.bn_aggr(out=mv[:, c, :], in_=stats[:, c, :]), reads=[sb_], writes=[sb_])
                kb.op("act", lambda: nc.scalar.activation(out=mv[:, c, 1:2], in_=mv[:, c, 1:2], func=AF.Sqrt, bias=EPS, scale=1.0),
                      reads=[sb_], writes=[sb_])
                kb.op("dve", lambda: nc.vector.reciprocal(out=mv[:, c, 1:2], in_=mv[:, c, 1:2]), reads=[sb_], writes=[sb_])
                kb.op("dve", lambda: nc.vector.tensor_scalar(out=vst[:, c, :], in0=vst[:, c, :], scalar1=mv[:, c, 0:1],
                                                             scalar2=mv[:, c, 1:2], op0=ALU.subtract, op1=ALU.mult),
                      reads=[sb_] + vb, writes=vb)
                kb.op("dve", lambda: nc.vector.tensor_tensor(out=vst[:, c, :], in0=vst[:, c, :], in1=lng[:, :], op=ALU.mult),
                      reads=vb + [B_ln], writes=vb)
                for hlf in range(2):
                    st, stb = ost()
                    kb.op("dve", lambda: nc.vector.tensor_tensor(out=st[:, :], in0=vst[:, c, hlf * 512:(hlf + 1) * 512],
                                                                 in1=lnb[:, hlf * 512:(hlf + 1) * 512], op=ALU.add),
                          reads=vb + [B_ln], writes=[stb])
                    kb.dma("sp", vn[T0 + c * 128:T0 + (c + 1) * 128, hlf * 512:(hlf + 1) * 512], st[:, :], reads=[stb], writes=[B_vn[g]])
            barrier()

    def phase_attn(l, g):
        T0 = g * GT
        lo = max(16 * g - 4, 0)
        hi = min(16 * g + 19, 63)
        nrows = hi - lo + 1
        ntok = nrows * 64
        ne, no = nrows // 2, (nrows - 1) // 2
        scale = 128.0 ** -0.5
        kdeps = [B_k[gg] for gg in range(NG) if not (16 * gg + 15 < lo or 16 * gg > hi)]
        vdeps = [B_V[gg] for gg in range(NG) if not (16 * gg + 15 < lo or 16 * gg > hi)]
        NB = 3
        with ExitStack() as es:
            def many(name, shape, dt, n):
                return [es.enter_context(nc.sbuf_tensor(uq("%s%d" % (name, i)), shape, dt)) for i in range(n)]
            Kh = many("Kh", [128, 1536], BF16, 2)
            qh = many("qh", [128, GT], BF16, 2)
            Ve = many("Ve", [128, 12, 128], BF16, 2)
            Vo = many("Vo", [128, 12, 128], BF16, 2)
            Bm = many("Bm", [64, 960], F32, 2)
            yh = many("yh", [128, GT], BF16, 2)
            ssb = many("ssb", [64, 512], F32, NB)
            pnb = many("pnb", [64, 512], BF16, NB)
            pTs = many("pTs", [128, 256], BF16, NB)
            st4 = many("st4", [64, 4], F32, NB)
            hb = [[Buf() for _ in range(6)] for _ in range(2)]
            wb = [[Buf() for _ in range(4)] for _ in range(NB)]
            sbank = {}

            def loads(h):
                hp = h % 2
                bK, bq, bVe, bVo, bBm, byh = hb[hp]
                kb.dma("sp", Kh[hp][:, 0:ntok], kT[h * 128:(h + 1) * 128, lo * 64:lo * 64 + ntok], reads=kdeps, writes=[bK])
                kb.dma("sp", qh[hp][:, :], qT[h * 128:(h + 1) * 128, T0:T0 + GT], reads=[B_q[g]], writes=[bq])
                kb.dma("sp", Ve[hp][:, 0:ne, :],
                       Vd[lo * 64:lo * 64 + ne * 128, h * 128:(h + 1) * 128].rearrange("(j p) d -> p j d", p=128),
                       reads=vdeps, writes=[bVe])
                kb.dma("sp", Vo[hp][:, 0:no, :],
                       Vd[lo * 64 + 64:lo * 64 + 64 + no * 128, h * 128:(h + 1) * 128].rearrange("(j p) d -> p j d", p=128),
                       reads=vdeps, writes=[bVo])
                kb.dma("sp", Bm[hp][:, :], sps[l][0:64, BIAS + h * 960:BIAS + (h + 1) * 960], writes=[bBm])
                kb.op("dve", lambda: nc.vector.tensor_tensor(out=Bm[hp][:, :], in0=Bm[hp][:, :], in1=maskt[:, :], op=ALU.add),
                      reads=[bBm, B_const], writes=[bBm])

            def geom(n):
                h, i = n // 16, n % 16
                r = 16 * g + i
                rs_ = min(max(r - 4, 0), 56)
                return h, i, h % 2, rs_ - lo, rs_ - r + 7

            def st0(n):
                h, i, hp, e0, ro0 = geom(n)
                if i == 9 and h + 1 < 16:
                    loads(h + 1)
                pb, pbb = psum[n % 3], psum_b[n % 3]
                sbank[n] = (pb, pbb)
                mm_group(pb[0:64, :], pbb, [(qh[hp][:, i * 64:(i + 1) * 64], Kh[hp][:, e0 * 64:e0 * 64 + 512])], [hb[hp][1], hb[hp][0]])

            def st1(n):
                h, i, hp, e0, ro0 = geom(n)
                pb, pbb = sbank.pop(n)
                ib = n % NB
                bs, bpn, bpT, bst = wb[ib]
                s_, st_ = ssb[ib], st4[ib]
                kb.op("dve", lambda: nc.vector.scalar_tensor_tensor(out=s_[:, :], in0=pb[0:64, :], scalar=scale,
                                                                    in1=Bm[hp][:, ro0 * 64:ro0 * 64 + 512], op0=ALU.mult, op1=ALU.add),
                      reads=[pbb, hb[hp][4]], writes=[bs])
                kb.op("dve", lambda: nc.vector.reduce_max(out=st_[:, 0:1], in_=s_[:, :], axis=AX.X), reads=[bs], writes=[bst])
                kb.op("dve", lambda: nc.vector.tensor_scalar(out=st_[:, 1:2], in0=st_[:, 0:1], scalar1=-1.0, scalar2=None, op0=ALU.mult),
                      reads=[bst], writes=[bst])

            def st2(n):
                ib = n % NB
                bs, bpn, bpT, bst = wb[ib]
                s_, st_ = ssb[ib], st4[ib]
                kb.op("act", lambda: nc.scalar.activation(out=s_[:, :], in_=s_[:, :], func=AF.Exp, bias=st_[:, 1:2], scale=1.0),
                      reads=[bs, bst], writes=[bs])
                kb.op("dve", lambda: nc.vector.reduce_sum(out=st_[:, 2:3], in_=s_[:, :], axis=AX.X), reads=[bs], writes=[bst])

            def st3(n):
                ib = n % NB
                bs, bpn, bpT, bst = wb[ib]
                st_ = st4[ib]
                kb.op("dve", lambda: nc.vector.reciprocal(out=st_[:, 3:4], in_=st_[:, 2:3]), reads=[bst], writes=[bst])

            def st4_(n):
                ib = n % NB
                bs, bpn, bpT, bst = wb[ib]
                if "actnorm" in flags:
                    kb.op("act", lambda: nc.scalar.activation(out=pnb[ib][:, :], in_=ssb[ib][:, :], func=AF.Copy, scale=st4[ib][:, 3:4]),
                          reads=[bs, bst], writes=[bpn])
                else:
                    kb.op("dve", lambda: nc.vector.tensor_scalar(out=pnb[ib][:, :], in0=ssb[ib][:, :], scalar1=st4[ib][:, 3:4], scalar2=None,
                                                                 op0=ALU.mult), reads=[bs, bst], writes=[bpn])

            def st5(n):
                ib = n % NB
                bs, bpn, bpT, bst = wb[ib]
                tb = psT_b[n % 2]
                psT = psTs[n % 2]
                c0 = 0
                kb._deps("pe", [bpn, B_const], [tb])
                ins = None
                for j in range(4):
                    ins = nc.tensor.transpose(out=psT[:, c0 + j * 64:c0 + (j + 1) * 64], in_=pnb[ib][:, j * 128:(j + 1) * 128],
                                              identity=ident[0:64, 0:64])
                s = kb.csem["pe"]
                s[1] += 1
                ins.then_inc(s[0][0], 1)
                ev = Ev(s[0][0], s[0][1], s[1])
                kb._post(ev, [bpn, B_const], [tb])

            def st6(n):
                ib = n % NB
                bs, bpn, bpT, bst = wb[ib]
                kb.op("act", lambda: nc.scalar.copy(out=pTs[ib][:, :], in_=psTs[n % 2][:, 0:256]), reads=[psT_b[n % 2]], writes=[bpT])

            def st7(n):
                h, i, hp, e0, ro0 = geom(n)
                ib = n % NB
                bs, bpn, bpT, bst = wb[ib]
                if e0 % 2 == 0:
                    Vx, bVx, j0 = Ve[hp], hb[hp][2], e0 // 2
                else:
                    Vx, bVx, j0 = Vo[hp], hb[hp][3], (e0 - 1) // 2
                ob = 3 + n % 2
                mm_group(psum[ob][:, 0:64], psum_b[ob], [(Vx[:, j0 + j, :], pTs[ib][:, j * 64:(j + 1) * 64]) for j in range(4)],
                         [bVx, bpT])

            def st8(n):
                h, i, hp, e0, ro0 = geom(n)
                ob = 3 + n % 2
                kb.op("act", lambda: nc.scalar.copy(out=yh[hp][:, i * 64:(i + 1) * 64], in_=psum[ob][:, 0:64]),
                      reads=[psum_b[ob]], writes=[hb[hp][5]])
                if i == 15:
                    kb.dma("sp", yT[h * 128:(h + 1) * 128, T0:T0 + GT], yh[hp][:, :], reads=[hb[hp][5]], writes=[B_y[h][g]])

            stages = [st0, st1, st2, st3, st4_, st5, st6, st7, st8]
            for f in flags:
                if f.startswith("ms"):
                    stages = stages[:int(f[2:])]
            NIT = 256
            loads(0)
            for step in range(NIT + len(stages) - 1):
                for k in range(len(stages) - 1, -1, -1):
                    n = step - k
                    if 0 <= n < NIT:
                        stages[k](n)
            barrier()

    def phase_pool(l, g):
        T0 = g * GT
        pdeps = [B_p[gg] for gg in (g - 1, g, g + 1) if 0 <= gg < NG] + [B_ppad]
        with ExitStack() as es:
            invc = es.enter_context(nc.sbuf_tensor(uq("invc"), [128, GT], F32))
            pe_ = [es.enter_context(nc.sbuf_tensor(uq("pe%d" % i), [128, 1040], F32)) for i in range(2)]
            sA = es.enter_context(nc.sbuf_tensor(uq("sA"), [128, 1040], F32))
            sB = es.enter_context(nc.sbuf_tensor(uq("sB"), [128, 1040], F32))
            dch = es.enter_context(nc.sbuf_tensor(uq("dch"), [128, 8, GT], BF16))
            binv, bsA, bsB = Buf(), Buf(), Buf()
            bpe = [Buf(), Buf()]
            bd = [Buf() for _ in range(8)]
            for cc in range(8):
                wi = cc // 2
                w = POOL_WINDOWS[wi]
                pt, pb_ = pe_[cc % 2], bpe[cc % 2]
                if cc % 2 == 0:
                    kb.dma("sp", invc[:, :], cst[:, CINV + wi * 4096 + T0:CINV + wi * 4096 + T0 + GT], writes=[binv])
                kb.dma("sp", pt[:, :], pT[cc * 128:(cc + 1) * 128, T0:T0 + 1040], reads=pdeps, writes=[pb_])
                src, sbf = pt, pb_
                L = 1040
                step = 1
                flip = 0
                while step < w:
                    dst, dbf = (sA, bsA) if flip == 0 else (sB, bsB)
                    flip ^= 1
                    L2 = L - step
                    kb.op("dve", lambda: nc.vector.tensor_tensor(out=dst[:, 0:L2], in0=src[:, 0:L2], in1=src[:, step:step + L2], op=ALU.add),
                          reads=[sbf], writes=[dbf])
                    src, sbf, L = dst, dbf, L2
                    step *= 2
                o0 = 8 - w // 2
                dst, dbf = (sA, bsA) if flip == 0 else (sB, bsB)
                kb.op("dve", lambda: nc.vector.tensor_tensor(out=dst[:, 0:GT], in0=src[:, o0:o0 + GT], in1=invc[:, :], op=ALU.mult),
                      reads=[sbf, binv], writes=[dbf])
                kb.op("dve", lambda: nc.vector.tensor_tensor(out=dch[:, cc, :], in0=dst[:, 0:GT], in1=pt[:, 8:8 + GT], op=ALU.subtract),
                      reads=[dbf, pb_], writes=[bd[cc]])
            for pg in range(4):
                for m in range(2):
                    oc = pg * 2 + m
                    for t in range(2):
                        pb, pbb = bank()
                        pairs = [(pwb[:, ((pg * 2 + kc) * 2 + m) * 128:((pg * 2 + kc) * 2 + m + 1) * 128],
                                  dch[:, pg * 2 + kc, t * TT:(t + 1) * TT]) for kc in range(2)]
                        mm_group(pb[:, :], pbb, pairs, [B_spc, bd[pg * 2], bd[pg * 2 + 1]])
                        st, stb = ost()
                        kb.op("dve", lambda: nc.vector.tensor_scalar(out=st[:, :], in0=pb[:, :], scalar1=spc[:, PS + oc:PS + oc + 1], scalar2=None,
                                                                     op0=ALU.mult), reads=[pbb, B_spc], writes=[stb])
                        kb.dma("sp", yT[2048 + oc * 128:2048 + (oc + 1) * 128, T0 + t * TT:T0 + (t + 1) * TT], st[:, :],
                               reads=[stb], writes=[B_y[16 + oc][g]])
            barrier()

    def phase_sg(l, g):
        T0 = g * GT
        with ExitStack() as es:
            vnb = es.enter_context(nc.sbuf_tensor(uq("vnb"), [128, 8, 1024], BF16))
            bsb = es.enter_context(nc.sbuf_tensor(uq("bsb"), [128, 2048], F32))
            ut = [es.enter_context(nc.sbuf_tensor(uq("ut%d" % i), [128, TT], BF16)) for i in range(2)]
            bvn = [Buf() for _ in range(8)]
            bbs = Buf()
            but = [Buf(), Buf()]
            kb.dma("sp", bsb[:, :], sps[l][:, BSB:BSB + 2048], writes=[bbs])
            for c in range(8):
                kb.dma("sp", vnb[:, c, :], vn[T0 + c * 128:T0 + (c + 1) * 128, :], reads=[B_vn[g]], writes=[bvn[c]])
            n = 0
            for ch in range(8):
                sgg = ch // 2
                for t in range(2):
                    u_, ub = ut[n % 2], but[n % 2]
                    n += 1
                    kb.dma("sp", u_[:, :], uT[ch * 128:(ch + 1) * 128, T0 + t * TT:T0 + (t + 1) * TT], reads=[B_u[g]], writes=[ub])
                    pb, pbb = bank()
                    kb._deps("pe", [bvn[t * 4 + cq] for cq in range(4)] + [B_spc], [pbb])
                    ins = None
                    for cq in range(4):
                        c = t * 4 + cq
                        ins = mm(pb[:, cq * 128:(cq + 1) * 128], lhsT=vnb[:, c, ch * 128:(ch + 1) * 128],
                                 rhs=wstb[:, sgg * 128:(sgg + 1) * 128], start=True, stop=True)
                    s = kb.csem["pe"]
                    s[1] += 1
                    ins.then_inc(s[0][0], 1)
                    ev = Ev(s[0][0], s[0][1], s[1])
                    kb._post(ev, [bvn[t * 4 + cq] for cq in range(4)] + [B_spc], [pbb])
                    tf, tfb = ostf()
                    kb.op("dve", lambda: nc.vector.tensor_tensor(out=tf[:, :], in0=pb[:, :], in1=bsb[:, sgg * 512:(sgg + 1) * 512], op=ALU.add),
                          reads=[pbb, bbs], writes=[tfb])
                    st, stb = ost()
                    kb.op("dve", lambda: nc.vector.tensor_tensor(out=st[:, :], in0=tf[:, :], in1=u_[:, :], op=ALU.mult),
                          reads=[tfb, ub], writes=[stb])
                    kb.dma("sp", yT[3072 + ch * 128:3072 + (ch + 1) * 128, T0 + t * TT:T0 + (t + 1) * TT], st[:, :],
                           reads=[stb], writes=[B_y[24 + ch][g]])
            barrier()

    def load_big(src, bufs, g, nk=KC, row0=0):
        T0 = g * GT
        for kc in range(nk):
            for t in range(2):
                kb.dma("sp", big[:, kc, t * TT:(t + 1) * TT], src[(row0 + kc) * 128:(row0 + kc + 1) * 128, T0 + t * TT:T0 + (t + 1) * TT],
                       reads=[bufs(kc)], writes=[big_b[kc][t]])

    def phase_C(l, g):
        T0 = g * GT
        with ExitStack() as es:
            hgb = es.enter_context(nc.sbuf_tensor(uq("hgb"), [128, 4, GT], BF16))
            bhg = Buf()
            for kc in range(4):
                kb.dma("sp", hgb[:, kc, :], hgT[kc * 128:(kc + 1) * 128, T0:T0 + GT], reads=[B_hg[g]], writes=[bhg])
            load_big(yT, lambda kc: B_y[kc][g], g)
            bri = gui = None
            for m in range(32):
                if m % 2 == 0:
                    if bri is not None:
                        ring.release(bri[0])
                    bri = ring.acquire(("br", l, m // 2))
                o = m % 2
                for br in range(3):
                    i = m * 3 + br
                    if i % 16 == 0:
                        if gui is not None:
                            ring.release(gui[0])
                        gui = ring.acquire(("gu", l, i // 16))
                    goff = (i % 16) * 4
                    ks = (range(0, 16), range(16, 24), range(24, 32))[br]
                    for t in range(2):
                        gp, gpb = bank()
                        mm_group(gp[:, :], gpb, [(gui[1][:, (goff + kc) * 128:(goff + kc + 1) * 128], hgb[:, kc, t * TT:(t + 1) * TT])
                                                 for kc in range(4)], [gui[2], bhg])
                        bp_, bpb = bank()
                        mm_group(bp_[:, :], bpb, [(bri[1][:, (o * KC + kc) * 128:(o * KC + kc + 1) * 128], big[:, kc, t * TT:(t + 1) * TT])
                                                  for kc in ks], [bri[2]] + [big_b[kc][t] for kc in ks])
                        sg_, sgb = ostf()
                        kb.op("act", lambda: nc.scalar.activation(out=sg_[:, :], in_=gp[:, :], func=AF.Sigmoid,
                                                                  bias=spc[:, GB + br * 32 + m:GB + br * 32 + m + 1], scale=1.0),
                              reads=[gpb, B_spc], writes=[sgb])
                        if br == 0:
                            kb.op("dve", lambda: nc.vector.tensor_tensor(out=accs[t][:, :], in0=bp_[:, :], in1=sg_[:, :], op=ALU.mult),
                                  reads=[bpb, sgb], writes=[acc_b[t]])
                        else:
                            kb.op("dve", lambda: nc.vector.tensor_tensor(out=sg_[:, :], in0=bp_[:, :], in1=sg_[:, :], op=ALU.mult),
                                  reads=[bpb, sgb], writes=[sgb])
                            if br == 1:
                                kb.op("dve", lambda: nc.vector.tensor_tensor(out=accs[t][:, :], in0=accs[t][:, :], in1=sg_[:, :], op=ALU.add),
                                      reads=[acc_b[t], sgb], writes=[acc_b[t]])
                            else:
                                st, stb = ost()
                                kb.op("dve", lambda: nc.vector.tensor_tensor(out=st[:, :], in0=accs[t][:, :], in1=sg_[:, :], op=ALU.add),
                                      reads=[acc_b[t], sgb], writes=[stb])
                                kb.dma("sp", mT[m * 128:(m + 1) * 128, T0 + t * TT:T0 + (t + 1) * TT], st[:, :], reads=[stb], writes=[B_m[m][g]])
            ring.release(bri[0])
            ring.release(gui[0])
            barrier()

    def resid_evac(l, g, m, x_src):
        T0 = g * GT

        def evac(t, pb, pbb):
            c0 = T0 + t * TT
            xt, xb = xst()
            kb.dma("sp", xt[:, :], x_src[m * 128:(m + 1) * 128, c0:c0 + TT], reads=[B_x[m][g][t]], writes=[xb])
            kb.op("dve", lambda: nc.vector.tensor_tensor(out=xt[:, :], in0=pb[:, :], in1=xt[:, :], op=ALU.add), reads=[pbb, xb], writes=[xb])
            kb.dma("sp", xs[m * 128:(m + 1) * 128, c0:c0 + TT], xt[:, :], reads=[xb], writes=[B_x[m][g][t]])
        return evac

    def phase_D(l, g):
        load_big(mT, lambda kc: B_m[kc][g], g)
        x_src = x_in if l == 0 else xs
        for ti in range(16):
            ri, slot, sbuf_ = ring.acquire(("wo", l, ti))
            for o in range(2):
                fm_chunk(slot, sbuf_, o * KC, KC, big_rhs, big_rb, resid_evac(l, g, 2 * ti + o, x_src))
            ring.release(ri)

    def phase_E(l, g):
        norm_to_big(xs, spc[:, FG:FG + 32], g)
        for j in range(NJ):
            ri, slot, sbuf_ = ring.acquire(("ff", l, j))
            for t in range(2):
                gp, gpb = bank()
                mm_group(gp[:, :], gpb, [(slot[:, kc * 128:(kc + 1) * 128], big[:, kc, t * TT:(t + 1) * TT]) for kc in range(KC)],
                         [sbuf_] + [big_b[kc][t] for kc in range(KC)])
                up, upb = bank()
                mm_group(up[:, :], upb, [(slot[:, (KC + kc) * 128:(KC + kc + 1) * 128], big[:, kc, t * TT:(t + 1) * TT]) for kc in range(KC)],
                         [sbuf_] + [big_b[kc][t] for kc in range(KC)])
                sg_, sgb = ostf()
                kb.op("act", lambda: nc.scalar.activation(out=sg_[:, :], in_=gp[:, :], func=AF.Silu), reads=[gpb], writes=[sgb])
                st, stb = ost()
                kb.op("dve", lambda: nc.vector.tensor_tensor(out=st[:, :], in0=up[:, :], in1=sg_[:, :], op=ALU.mult),
                      reads=[upb, sgb], writes=[stb])
                kb.dma("sp", actT[j * 128:(j + 1) * 128, t * TT:(t + 1) * TT], st[:, :], reads=[stb], writes=[B_act[j]])
            ring.release(ri)

    def phase_F(l, g):
        k0 = 0
        for s, n in enumerate(DOWN_SPLIT):
            for kk in range(n):
                for t in range(2):
                    kb.dma("sp", big[:, kk, t * TT:(t + 1) * TT], actT[(k0 + kk) * 128:(k0 + kk + 1) * 128, t * TT:(t + 1) * TT],
                           reads=[B_act[k0 + kk]], writes=[big_b[kk][t]])
            for mp in range(16):
                ri, slot, sbuf_ = ring.acquire(("dn", l, s, mp))
                for o in range(2):
                    fm_chunk(slot, sbuf_, o * n, n, big_rhs, big_rb, resid_evac(l, g, 2 * mp + o, xs))
                ring.release(ri)
            k0 += n

    def phase_final(g):
        def f(kc, t, xt, xb, gcol, rs, rsb):
            kb.op("dve", lambda: nc.vector.scalar_tensor_tensor(out=xt[:, :], in0=xt[:, :], scalar=gcol, in1=rs[:, :],
                                                                op0=ALU.mult, op1=ALU.mult), reads=[xb, rsb, B_const], writes=[xb])
            kb.dma("sp", outT[kc * 128:(kc + 1) * 128, g * GT + t * TT:g * GT + (t + 1) * TT], xt[:, :], reads=[xb], writes=[B_out])
        rms_to(xs, fng, g, f)

    B_out = Buf()
    if mode == "attn_only":
        kb.dma("sp", spc[:, :], sps[0][:, 0:168], writes=[B_spc])
        zz, zzb = ost()
        kb.op("dve", lambda: nc.vector.memset(zz[:, :], 0.0), writes=[zzb])
        for r0 in range(0, 2048, 128):
            for c0 in range(0, 2048, TT):
                kb.dma("sp", qT[r0:r0 + 128, c0:c0 + TT], zz[:, :], reads=[zzb], writes=[B_q[0]])
                kb.dma("sp", kT[r0:r0 + 128, c0:c0 + TT], zz[:, :], reads=[zzb], writes=[B_k[0]])
                kb.dma("sp", Vd[r0:r0 + 128, c0:c0 + TT], zz[:, :], reads=[zzb], writes=[B_V[0]])
        phase_attn(0, 0)
        n_layers = 0
    for l in range(n_layers):
        if l > 0:
            kb.new_epoch()
        sp_ = sps[l]
        kb.dma("sp", spc[:, :], sp_[:, 0:168], writes=[B_spc])
        kb.dma("pool", pwb[:, :], sp_[:, PW:PW + 2048], writes=[B_spc])
        kb.dma("pool", wstb[:, :], sp_[:, WST:WST + 512], writes=[B_spc])
        for g in range(NG):
            phase_A(l, g)
        for g in range(NG):
            phase_attn(l, g)
            phase_pool(l, g)
            phase_sg(l, g)
            phase_C(l, g)
            phase_D(l, g)
            phase_E(l, g)
            phase_F(l, g)
    if mode is None:
        for g in range(NG):
            phase_final(g)
    else:
        zt2, zb2 = ostf()
        kb.op("dve", lambda: nc.vector.memset(zt2[:, :], 1.0), writes=[zb2])
        kb.dma("sp", outT[0:128, 0:TT], zt2[:, :], reads=[zb2], writes=[B_out])
    for name, (dst, src) in dbg.items():
        barrier()
        nr = src.shape[0]
        for r0 in range(0, nr, 1024):
            r1 = min(nr, r0 + 1024)
            kb.dma("sp", dst[r0:r1, :], src[r0:r1, :], writes=[B_out])
    pool = kb.dsem["sp"]
    for (sem, sid), c in zip(pool[0], pool[1]):
        if c > 0:
            kb._wait("sp", Ev(sem, sid, c))
    for e in ("pe", "act", "dve"):
        s = kb.csem[e]
        if s[1] > 0:
            kb._wait("sp", Ev(s[0][0], s[0][1], s[1]))
    return nc


def _tiles_fm(M, cols, KCn=KC):
    K = M.shape[0]
    assert K == KCn * 128
    idx = np.concatenate([np.arange(c, c + 128) for c in cols])
    sub = M[:, idx]
    nch = len(cols)
    a = sub.reshape(KCn, 128, nch // 2, 2, 128).transpose(2, 1, 3, 0, 4)
    return np.ascontiguousarray(a).reshape(nch // 2 * 128, 2 * KCn * 128)


def _layer_weights(inp, l):
    out = np.zeros((NTILES * 128, 8192), np.float32)
    r = 0

    def put(a):
        nonlocal r
        out[r:r + a.shape[0], :a.shape[1]] = a
        r += a.shape[0]
    w_in = inp["w_in"][l]
    gd = inp["gate_down"][l]
    cols = [j * 128 for j in range(32)] + [6144 + j * 128 for j in range(8)] + [7168 + j * 128 for j in range(8)]
    put(_tiles_fm(w_in, cols))
    put(_tiles_fm(gd, [j * 128 for j in range(4)]))
    for cg in range(12):
        c0 = 4096 + cg * 256 if cg < 8 else 8192 + (cg - 8) * 256
        a = w_in[:, c0:c0 + 256].reshape(KC, 128, 256).transpose(1, 0, 2)
        put(np.ascontiguousarray(a).reshape(128, KC * 256))
    wb = _tiles_fm(inp["w_branch"][l], [m * 128 for m in range(32)])
    gu = inp["gate_up"][l]
    gidx = np.concatenate([np.arange(br * 4096 + m * 128, br * 4096 + m * 128 + 128) for m in range(32) for br in range(3)])
    gsub = gu[:, gidx].reshape(4, 128, 6, 16, 128).transpose(2, 1, 3, 0, 4)
    gsub = np.ascontiguousarray(gsub).reshape(6 * 128, 8192)
    for m in range(32):
        if m % 2 == 0:
            put(wb[(m // 2) * 128:(m // 2 + 1) * 128])
        for br in range(3):
            i = m * 3 + br
            if i % 16 == 0:
                put(gsub[(i // 16) * 128:(i // 16 + 1) * 128])
    put(_tiles_fm(inp["w_out"][l], [m * 128 for m in range(32)]))
    wg, wu = inp["w_ffn_gate"][l], inp["w_ffn_up"][l]
    a = np.stack([wg.reshape(KC, 128, NJ, 128), wu.reshape(KC, 128, NJ, 128)], 0).transpose(3, 2, 0, 1, 4)
    put(np.ascontiguousarray(a).reshape(NJ * 128, 8192))
    wd = inp["w_ffn_down"][l]
    k0 = 0
    for n in DOWN_SPLIT:
        put(_tiles_fm(wd[k0 * 128:(k0 + n) * 128], [m * 128 for m in range(32)], n))
        k0 += n
    assert r == NTILES * 128
    return out


def _layer_small(inp, l):
    sp = np.zeros((128, NS), np.float32)
    fm = lambda v: v.reshape(-1, 128).T
    sp[:, AG:AG + 32] = fm(inp["attn_norm_g"][l])
    sp[:, FG:FG + 32] = fm(inp["ffn_norm_g"][l])
    sp[:, GB:GB + 96] = fm(inp["gate_b"][l])
    sp[:, PS:PS + 8] = fm(inp["pool_scale"][l])
    sp[:, LNG:LNG + 1024] = inp["gmlp_ln_g"][l][None, :]
    sp[:, LNB:LNB + 1024] = inp["gmlp_ln_b"][l][None, :]
    bs = inp["gmlp_b_s"][l]
    sp[:, BSB:BSB + 2048] = np.tile(bs[:, None, :], (1, 4, 1)).reshape(1, 2048)
    pw = inp["pool_w"][l]
    sp[:, PW:PW + 2048] = pw.reshape(4, 2, 128, 2, 128).transpose(2, 0, 1, 3, 4).reshape(128, 2048)
    ws = inp["gmlp_w_s"][l]
    sp[:, WST:WST + 512] = ws.transpose(2, 0, 1).reshape(128, 512)
    rpb = inp["rpb"][l]
    cq = np.arange(64)[:, None]
    ck = np.arange(64)[None, :]
    co = np.clip(ck - cq, -15, 15) + 15
    bias = rpb[:, :, co]
    sp[0:64, BIAS:BIAS + 16 * 960] = bias.transpose(2, 0, 1, 3).reshape(64, 16 * 960)
    return sp


def _consts(inp):
    c = np.zeros((128, NCST), np.float32)
    c[:, CID:CID + 128] = np.eye(128, dtype=np.float32)
    t = np.arange(T)
    for wi, w in enumerate(POOL_WINDOWS):
        lo = np.clip(t - w // 2, 0, T - 1)
        hi = np.clip(t - w // 2 + w - 1, 0, T - 1)
        c[:, CINV + wi * T:CINV + (wi + 1) * T] = (1.0 / (hi - lo + 1).astype(np.float32))[None, :]
    c[:, CFNG:CFNG + 32] = inp["final_norm_g"].reshape(-1, 128).T
    cq = np.arange(64)[:, None]
    ck = np.arange(64)[None, :]
    cs = np.clip(cq - 8, 0, 48)
    valid = (ck >= cs) & (ck < cs + 16)
    m = np.where(valid, 0.0, -30000.0).astype(np.float32)
    c[0:64, CMASK:CMASK + 960] = np.tile(m[:, None, :], (1, 15, 1)).reshape(64, 960)
    return c


_CACHE = {}


def kernel(**inputs):
    inp = {k: np.asarray(v) for k, v in inputs.items()}
    n_layers = inp["w_in"].shape[0]
    ncores = inp["x"].shape[0]
    if "nc" not in _CACHE:
        _CACHE["nc"] = build_program(n_layers)
    nc = _CACHE["nc"]
    shared = {"cst": _consts(inp)}
    for l in range(n_layers):
        shared["w%d" % l] = _layer_weights(inp, l)
        shared["sp%d" % l] = _layer_small(inp, l)
    in_maps = []
    for c in range(ncores):
        m = dict(shared)
        m["xT"] = np.ascontiguousarray(inp["x"][c].T)
        in_maps.append(m)
    res = run_bass_kernel_spmd(nc, in_maps, core_ids=list(range(ncores)))
    out = np.stack([np.ascontiguousarray(np.asarray(res.results[c]["outT"]).T) for c in range(ncores)], 0)
    return out.astype(np.float32)
```
